# Optimizing a Trainium2 kernel written in Bass

```python
import math
import jax, jax.numpy as jnp
from jax import lax
import numpy as np

D_MODEL = 1024
BATCH = 16
SEQ = 2048
DEPTH = 4
DEC_BATCH = 8
DEC_SEQ = 2048
PAST_LEN = 128

HEAD_DIM = 64
N_MIXERS = 3
A_HEADS = 16
A_KV_HEADS = 4
A_RADIUS = 128
B_CHUNK = 128
B_HIDDEN = 2 * D_MODEL
B_GROUPS = 8
C_GROUPS = ((128, 1), (512, 4), (2048, 16))
C_HEADS = 16
C_KV_HEADS = 4
D_FF = 4 * D_MODEL
NUM_BUCKETS = 32
REL_MAX_DISTANCE = 1024
BIAS_HEADS = 16
RMS_EPS = 1e-6
LN_EPS = 1e-5
N_A = len(range(0, DEPTH, N_MIXERS))
N_B = len(range(1, DEPTH, N_MIXERS))
N_C = len(range(2, DEPTH, N_MIXERS))
A_QKV = (A_HEADS + 2 * A_KV_HEADS) * HEAD_DIM
C_GROUP_QKV = (C_HEADS + 2 * C_KV_HEADS) * HEAD_DIM
C_QKV = len(C_GROUPS) * C_GROUP_QKV

kernel_name = "hybrid_bidir_encoder_window_gmlp_dilated"


def rmsnorm(x, g):
    xf = x.astype(jnp.float32)
    y = xf * lax.rsqrt(jnp.mean(xf * xf, axis=-1, keepdims=True) + RMS_EPS)
    return (y * g.astype(jnp.float32)).astype(x.dtype)


def _rel_bucket(rel):
    half = NUM_BUCKETS // 2
    max_exact = half // 2
    n = np.abs(rel)
    large = max_exact + (np.log(np.maximum(n, 1) / max_exact) / np.log(REL_MAX_DISTANCE / max_exact) * (half - max_exact)).astype(np.int32)
    large = np.minimum(large, half - 1)
    return (rel > 0).astype(np.int32) * half + np.where(n < max_exact, n, large)


def _band_bias(rel_bias, blk, radius, dilation):
    width = blk + 2 * radius
    rel = (np.arange(width)[None, :] - radius - np.arange(blk)[:, None]) * dilation
    return jnp.transpose(rel_bias[_rel_bucket(rel)], (2, 0, 1))


def banded_attention(q, k, v, bias, radius, sink=None):
    n, L, hq, hd = q.shape
    hkv = k.shape[2]
    rep = hq // hkv
    blk = math.gcd(radius, L)
    nb = L // blk
    width = blk + 2 * radius
    idx = np.arange(nb)[:, None] * blk + np.arange(width)[None, :]
    key_pos = idx - radius
    rel = np.arange(width)[None, :] - radius - np.arange(blk)[:, None]
    mask = ((key_pos >= 0) & (key_pos < L))[:, None, :] & (np.abs(rel) <= radius)[None]
    pad = ((0, 0), (radius, radius), (0, 0), (0, 0))
    kb = jnp.take(jnp.pad(k, pad), idx.reshape(-1), axis=1).reshape(n, nb, width, hkv, hd)
    vb = jnp.take(jnp.pad(v, pad), idx.reshape(-1), axis=1).reshape(n, nb, width, hkv, hd)
    qb = q.reshape(n, nb, blk, hkv, rep, hd)
    s = jnp.einsum('bnqgrd,bnkgd->bngrqk', qb, kb, preferred_element_type=jnp.float32) * (hd ** -0.5)
    s = s + bias.reshape(hkv, rep, blk, width).astype(jnp.float32)
    s = jnp.where(mask[None, :, None, None], s, -jnp.inf)
    m = jnp.max(s, axis=-1, keepdims=True)
    if sink is not None:
        sk = sink.reshape(hkv, rep, 1, 1).astype(jnp.float32)
        m = jnp.maximum(m, sk)
    p = jnp.exp(s - m)
    denom = jnp.sum(p, axis=-1, keepdims=True)
    if sink is not None:
        denom = denom + jnp.exp(sk - m)
    o = jnp.einsum('bngrqk,bnkgd->bngrqd', p.astype(v.dtype), vb, preferred_element_type=jnp.float32) / denom
    o = jnp.transpose(o, (0, 1, 4, 2, 3, 5)).reshape(n, L, hq, hd).astype(q.dtype)
    lse = jnp.transpose((m + jnp.log(denom))[..., 0], (0, 1, 4, 2, 3)).reshape(n, L, hq)
    return o, lse


def windowed_sink_gqa(h, w_qkv, sink, w_o, rel_bias):
    B, S, _ = h.shape
    q, k, v = jnp.split(h @ w_qkv, [A_HEADS * HEAD_DIM, (A_HEADS + A_KV_HEADS) * HEAD_DIM], axis=-1)
    q = q.reshape(B, S, A_HEADS, HEAD_DIM)
    k = k.reshape(B, S, A_KV_HEADS, HEAD_DIM)
    v = v.reshape(B, S, A_KV_HEADS, HEAD_DIM)
    bias = _band_bias(rel_bias, math.gcd(A_RADIUS, S), A_RADIUS, 1)
    o, _ = banded_attention(q, k, v, bias, A_RADIUS, sink)
    return o.reshape(B, S, A_HEADS * HEAD_DIM) @ w_o


def spatial_gating_mlp(h, w_in, ln_g, ln_b, w_s, b_s, w_out):
    B, S, _ = h.shape
    z = jax.nn.gelu(h @ w_in, approximate=False)
    u, v = jnp.split(z, 2, axis=-1)
    vf = v.astype(jnp.float32)
    mu = jnp.mean(vf, axis=-1, keepdims=True)
    var = jnp.mean(jnp.square(vf - mu), axis=-1, keepdims=True)
    v = ((vf - mu) * lax.rsqrt(var + LN_EPS) * ln_g.astype(jnp.float32) + ln_b.astype(jnp.float32)).astype(h.dtype)
    vc = v.reshape(B, S // B_CHUNK, B_CHUNK, B_GROUPS, B_HIDDEN // B_GROUPS)
    mixed = jnp.einsum('gpq,bcqge->bcpge', w_s, vc) + jnp.transpose(b_s)[:, :, None]
    return (u * mixed.reshape(B, S, B_HIDDEN)) @ w_out


def dilated_mixture_attention(h, w_qkv, w_o, rel_bias):
    B, S, _ = h.shape
    groups = jnp.split(h @ w_qkv, len(C_GROUPS), axis=-1)
    outs, lses = [], []
    for (window, dil), pg in zip(C_GROUPS, groups):
        q, k, v = jnp.split(pg, [C_HEADS * HEAD_DIM, (C_HEADS + C_KV_HEADS) * HEAD_DIM], axis=-1)
        radius = window // (2 * dil)
        L = S // dil

        def to_sub(t, heads):
            t = t.reshape(B, L, dil, heads, HEAD_DIM)
            return jnp.transpose(t, (0, 2, 1, 3, 4)).reshape(B * dil, L, heads, HEAD_DIM)

        bias = _band_bias(rel_bias, math.gcd(radius, L), radius, dil)
        o, lse = banded_attention(to_sub(q, C_HEADS), to_sub(k, C_KV_HEADS), to_sub(v, C_KV_HEADS), bias, radius)
        outs.append(jnp.transpose(o.reshape(B, dil, L, C_HEADS, HEAD_DIM), (0, 2, 1, 3, 4)).reshape(B, S, C_HEADS, HEAD_DIM))
        lses.append(jnp.transpose(lse.reshape(B, dil, L, C_HEADS), (0, 2, 1, 3)).reshape(B, S, C_HEADS))
    wts = jax.nn.softmax(jnp.stack(lses, axis=0), axis=0)
    o = jnp.einsum('gbsh,gbshd->bshd', wts, jnp.stack(outs, axis=0).astype(jnp.float32)).astype(h.dtype)
    return o.reshape(B, S, C_HEADS * HEAD_DIM) @ w_o


def squared_relu_mlp(h, w1, w2):
    return jnp.square(jax.nn.relu(h @ w1)) @ w2


def trunk(x, rel_bias, norm_mix_g, norm_ffn_g, final_g, ffn_w1, ffn_w2, a_wqkv, a_sink, a_wo,
          b_win, b_ln_g, b_ln_b, b_ws, b_bs, b_wo, c_wqkv, c_wo):
    for i in range(DEPTH):
        kind, j = i % N_MIXERS, i // N_MIXERS
        h = rmsnorm(x, norm_mix_g[i])
        if kind == 0:
            m = windowed_sink_gqa(h, a_wqkv[j], a_sink[j], a_wo[j], rel_bias)
        elif kind == 1:
            m = spatial_gating_mlp(h, b_win[j], b_ln_g[j], b_ln_b[j], b_ws[j], b_bs[j], b_wo[j])
        else:
            m = dilated_mixture_attention(h, c_wqkv[j], c_wo[j], rel_bias)
        x = x + m
        x = x + squared_relu_mlp(rmsnorm(x, norm_ffn_g[i]), ffn_w1[i], ffn_w2[i])
    return rmsnorm(x, final_g)


def setup_inputs(seed: int = 0) -> dict:
    key = jax.random.key(seed)
    ks = jax.random.split(key, 20)

    def nrm(k, shape, scale):
        return jax.random.normal(k, shape, jnp.float32) * scale

    return {
        "x_prompt": nrm(ks[0], (BATCH, SEQ, D_MODEL), 1.0),
        "x_sample": nrm(ks[1], (DEC_BATCH, DEC_SEQ, D_MODEL), 1.0),
        "rel_bias": nrm(ks[2], (NUM_BUCKETS, BIAS_HEADS), 0.5),
        "norm_mix_g": 1.0 + nrm(ks[3], (DEPTH, D_MODEL), 0.05),
        "norm_ffn_g": 1.0 + nrm(ks[4], (DEPTH, D_MODEL), 0.05),
        "final_g": 1.0 + nrm(ks[5], (D_MODEL,), 0.05),
        "ffn_w1": nrm(ks[6], (DEPTH, D_MODEL, D_FF), D_MODEL ** -0.5),
        "ffn_w2": nrm(ks[7], (DEPTH, D_FF, D_MODEL), 0.5 * D_FF ** -0.5),
        "a_wqkv": nrm(ks[8], (N_A, D_MODEL, A_QKV), D_MODEL ** -0.5),
        "a_sink": nrm(ks[9], (N_A, A_HEADS), 0.5),
        "a_wo": nrm(ks[10], (N_A, A_HEADS * HEAD_DIM, D_MODEL), (A_HEADS * HEAD_DIM) ** -0.5),
        "b_win": nrm(ks[11], (N_B, D_MODEL, 2 * B_HIDDEN), D_MODEL ** -0.5),
        "b_ln_g": 1.0 + nrm(ks[12], (N_B, B_HIDDEN), 0.05),
        "b_ln_b": nrm(ks[13], (N_B, B_HIDDEN), 0.02),
        "b_ws": nrm(ks[14], (N_B, B_GROUPS, B_CHUNK, B_CHUNK), B_CHUNK ** -0.5),
        "b_bs": 1.0 + nrm(ks[15], (N_B, B_GROUPS, B_CHUNK), 0.1),
        "b_wo": nrm(ks[16], (N_B, B_HIDDEN, D_MODEL), B_HIDDEN ** -0.5),
        "c_wqkv": nrm(ks[17], (N_C, D_MODEL, C_QKV), D_MODEL ** -0.5),
        "c_wo": nrm(ks[18], (N_C, C_HEADS * HEAD_DIM, D_MODEL), (C_HEADS * HEAD_DIM) ** -0.5),
    }


def reference(x_prompt, x_sample, rel_bias, norm_mix_g, norm_ffn_g, final_g, ffn_w1, ffn_w2,
              a_wqkv, a_sink, a_wo, b_win, b_ln_g, b_ln_b, b_ws, b_bs, b_wo, c_wqkv, c_wo):
    y_prompt = trunk(x_prompt, rel_bias, norm_mix_g, norm_ffn_g, final_g, ffn_w1, ffn_w2, a_wqkv, a_sink, a_wo,
                     b_win, b_ln_g, b_ln_b, b_ws, b_bs, b_wo, c_wqkv, c_wo)
    y_sample = trunk(x_sample, rel_bias, norm_mix_g, norm_ffn_g, final_g, ffn_w1, ffn_w2, a_wqkv, a_sink, a_wo,
                     b_win, b_ln_g, b_ln_b, b_ws, b_bs, b_wo, c_wqkv, c_wo)
    return (y_prompt, y_sample)
```

```python
import contextlib
import numpy as np
import concourse.bass as bass
import concourse.mybir as mybir
from concourse.bass_utils import run_bass_kernel_spmd

F32 = mybir.dt.float32
BF16 = mybir.dt.bfloat16
ALU = mybir.AluOpType
AF = mybir.ActivationFunctionType

T = 2048
NCORES = 8
NSLOT = 3
RMS_EPS = 1e-6
LN_EPS = 1e-5
NOILV = False
DBG = 99
SKIPMIX = False


def _rel_bucket(rel):
    half = 16
    max_exact = 8
    n = np.abs(rel)
    large = max_exact + (np.log(np.maximum(n, 1) / max_exact) / np.log(1024 / max_exact) * (half - max_exact)).astype(np.int32)
    large = np.minimum(large, half - 1)
    return (rel > 0).astype(np.int32) * half + np.where(n < max_exact, n, large)


def _onehot_tables():
    out = np.zeros((33, 4, 512), np.float32)
    for t, (radius, dil) in enumerate(((128, 1), (64, 1), (64, 4), (64, 16))):
        rel = 255 - np.arange(512)
        b = _rel_bucket(rel * dil)
        valid = np.abs(rel) <= radius
        for u in range(512):
            out[b[u] if valid[u] else 32, t, u] = 1.0
    return out


class S:
    ENG = ('pe', 'act', 'dve', 'pool', 'sp')

    def __init__(self, nc, stack):
        self.nc = nc
        self.stack = stack
        self.ops = {e: [] for e in self.ENG}
        self.sems = {}
        self.cnt = {}
        self.res = {}
        self.waited = {e: {} for e in self.ENG}
        self.floor = {e: {} for e in self.ENG}
        for e in ('pe', 'act', 'dve', 'pool'):
            self.mksem(e)
        self.pe_open = False
        self.bar = {}
        self.psum = [nc.alloc_psum_tensor("ps%d" % i, [128, 512], F32) for i in range(8)]
        self.psi = 0
        self.nsem = 0

    def mksem(self, key):
        self.sems[key] = self.stack.enter_context(self.nc.semaphore("sm%d" % len(self.sems)))
        self.cnt[key] = 0

    def ps(self):
        i = self.psi % 8
        self.psi += 1
        return self.psum[i], ('ps', i)

    def _waits(self, eng, reads, writes):
        need = dict(self.floor[eng])

        def add(ev):
            if ev is None:
                return
            k, v = ev
            if need.get(k, 0) < v:
                need[k] = v
        for k in reads:
            r = self.res.get(k)
            if r is not None:
                add(r[0])
        for k in writes:
            r = self.res.get(k)
            if r is not None:
                add(r[0])
                for kk, vv in r[1].items():
                    add((kk, vv))
        out = []
        wd = self.waited[eng]
        for k, v in need.items():
            if eng == 'pe' and k == 'pe':
                continue
            if wd.get(k, 0) >= v:
                continue
            wd[k] = v
            out.append((k, v))
        self.floor[eng] = {}
        return out

    def _commit(self, ev, reads, writes):
        k, v = ev
        for key in reads:
            r = self.res.setdefault(key, [None, {}])
            if r[1].get(k, 0) < v:
                r[1][k] = v
        for key in writes:
            self.res[key] = [ev, {}]

    def op(self, eng, fn, reads=(), writes=(), signal=True):
        waits = self._waits(eng, reads, writes)
        if signal:
            self.cnt[eng] += 1
            ev = (eng, self.cnt[eng])
            if eng == 'pe':
                self.pe_open = False
        else:
            ev = (eng, self.cnt[eng] + 1)
            self.pe_open = True
        self.ops[eng].append((fn, waits, eng if signal else None, 1))
        self._commit(ev, reads, writes)

    def dma(self, q, semkey, fns, reads=(), writes=(), union=False):
        if semkey not in self.sems:
            self.mksem(semkey)
        if union:
            f = self.floor[q]
            for k, v in self.bar.items():
                if f.get(k, 0) < v:
                    f[k] = v
        waits = self._waits(q, reads, writes)
        for i, fn in enumerate(fns):
            self.cnt[semkey] += 16
            self.ops[q].append((fn, waits if i == 0 else [], semkey, 16))
        self._commit((semkey, self.cnt[semkey]), reads, writes)

    def barrier(self, dma_keys=()):
        assert not self.pe_open
        cur = {e: self.cnt[e] for e in ('pe', 'act', 'dve', 'pool')}
        for k in dma_keys:
            if k in self.cnt:
                cur[k] = self.cnt[k]
        for e in ('pe', 'act', 'dve', 'pool'):
            f = self.floor[e]
            for k, v in cur.items():
                if k != e and f.get(k, 0) < v:
                    f[k] = v
        for k, v in cur.items():
            if self.bar.get(k, 0) < v:
                self.bar[k] = v

    def final_wait(self, eng, keys):
        waits = [(k, self.cnt[k]) for k in keys if k in self.cnt]
        self.ops[eng].append((None, waits, None, 0))

    def emit(self, block):
        def run(name):
            def f(eng):
                for fn, waits, sk, inc in self.ops[name]:
                    for k, v in waits:
                        eng.wait_ge(self.sems[k], v)
                    if fn is None:
                        continue
                    ins = fn(eng)
                    if sk is not None:
                        ins.then_inc(self.sems[sk], inc)
            return f
        block.tensor(run('pe'))
        block.scalar(run('act'))
        block.vector(run('dve'))
        block.gpsimd(run('pool'))
        block.sync(run('sp'))

    def mmg(self, out_ap, key, pairs, first=True, last=True):
        n = len(pairs)
        for i, (l, r, rd) in enumerate(pairs):
            st = first and i == 0
            sp = last and i == n - 1
            self.op('pe', (lambda e, l=l, r=r, st=st, sp=sp: e.matmul(out_ap, lhsT=l, rhs=r, start=st, stop=sp)),
                    reads=rd, writes=[key], signal=(i == n - 1))


def build(nseq=3, layers=(0, 1, 2, 3)):
    nc = bass.Bass("TRN2", target_bir_lowering=False)
    stack = contextlib.ExitStack()

    def din(name, shape, dtype=F32):
        return nc.dram_tensor(name, list(shape), dtype, kind="ExternalInput").ap()

    x = din("x", [nseq, T, 1024])
    y = nc.dram_tensor("y", [nseq, T, 1024], F32, kind="ExternalOutput").ap()
    rel_bias = din("rel_bias", [32, 16])
    prm_in = din("prm", [13, 1024])
    ffn_w1 = din("ffn_w1", [4, 1024, 4096])
    ffn_w2 = din("ffn_w2", [4, 4096, 1024])
    a_wqkv = din("a_wqkv", [2, 1024, 1536])
    a_sink = din("a_sink", [1, 32])
    a_wo = din("a_wo", [2, 1024, 1024])
    b_win = din("b_win", [1, 1024, 4096])
    b_ws = din("b_ws", [8, 128, 128])
    b_bs = din("b_bs", [1, 1024])
    b_wo = din("b_wo", [1, 2048, 1024])
    c_wqkv = din("c_wqkv", [1, 1024, 4608])
    c_wo = din("c_wo", [1, 1024, 1024])
    ident_in = din("ident", [128, 128])
    oh_in = din("oh", [33, 4 * 512])

    def dscr(name, shape, dtype=BF16):
        return nc.dram_tensor(name, list(shape), dtype, kind="Internal").ap()

    w1b = dscr("w1b", [4, 1024, 4096])
    w2b = dscr("w2b", [4, 4096, 1024])
    aqkvb = dscr("aqkvb", [2, 1024, 1536])
    awob = dscr("awob", [2, 1024, 1024])
    bwinb = dscr("bwinb", [1, 1024, 4096])
    bwob = dscr("bwob", [1, 2048, 1024])
    cqkvb = dscr("cqkvb", [1, 1024, 4608])
    cwob = dscr("cwob", [1, 1024, 1024])
    tabs_t = nc.dram_tensor("tabs", [4, 16, 128, 512], BF16, kind="Internal")
    tabs = tabs_t.ap()

    xT = nc.alloc_sbuf_tensor("xT", [128, 8, T], F32)
    hT = nc.alloc_sbuf_tensor("hT", [128, 8, T], BF16)
    colprm = nc.alloc_sbuf_tensor("colprm", [128, 8, 16], F32)
    ident = nc.alloc_sbuf_tensor("identf", [128, 128], F32)
    ones = nc.alloc_sbuf_tensor("onesb", [128, 128], BF16)
    es = nc.alloc_sbuf_tensor("es", [128, 32], F32)
    WsT = nc.alloc_sbuf_tensor("WsT", [128, 8, 128], BF16)
    Cterm = nc.alloc_sbuf_tensor("Cterm", [128, 16, 128], F32)
    UBYTES = 102400
    U = nc.alloc_sbuf_tensor("U", [128, UBYTES // 2], BF16)

    def uv(off, shape, dtype):
        n = 1
        for d in shape:
            n *= d
        nb = n * (4 if dtype == F32 else 2)
        assert off % 4 == 0 and off + nb <= UBYTES, (off, nb)
        a = U[:, off // 2:(off + nb) // 2]
        if dtype == F32:
            a = a.bitcast(F32)
        if len(shape) == 2:
            return a.rearrange("p (a b) -> p a b", a=shape[0])
        if len(shape) == 3:
            return a.rearrange("p (a b c) -> p a b c", a=shape[0], b=shape[1])
        return a

    wslot = [uv(i * 8192, [8, 512], BF16) for i in range(NSLOT)]
    R = NSLOT * 8192

    with stack:
        s = S(nc, stack)
        block = stack.enter_context(nc.Block())
        s.wsi = 0

        cast_q = []

        def cast(unit, dst2d, src2d, rows, cols):
            step = max(1, (256 * 1024) // cols)
            chunks = [(r0, min(rows, r0 + step)) for r0 in range(0, rows, step)]
            for ci, (r0, r1) in enumerate(chunks):
                cast_q.append((unit, (lambda e, r0=r0, r1=r1: e.dma_start(out=dst2d[r0:r1, :], in_=src2d[r0:r1, :])), ci == len(chunks) - 1))

        def cast_tick(n=1):
            for _ in range(n):
                if not cast_q:
                    return
                unit, fn, last = cast_q.pop(0)
                key = ('cast', unit)
                s.dma('pool', key, [fn])
                if last:
                    s.res[('wb', unit)] = [(key, s.cnt[key]), {}]

        def cast_need(unit):
            while ('wb', unit) not in s.res:
                assert cast_q
                cast_tick(1)

        def cast_layer(i):
            kind, j = i % 3, i // 3
            if kind == 0:
                cast(('aqkv', j), aqkvb[j], a_wqkv[j], 1024, 1536)
                cast(('awo', j), awob[j], a_wo[j], 1024, 1024)
            elif kind == 1:
                cast(('bwin', 0), bwinb[0], b_win[0], 1024, 4096)
                cast(('bwo', 0), bwob[0], b_wo[0], 2048, 1024)
            else:
                cast(('cqkv', 0), cqkvb[0], c_wqkv[0], 1024, 4608)
                cast(('cwo', 0), cwob[0], c_wo[0], 1024, 1024)
            cast(('w1', i), w1b[i], ffn_w1[i], 1024, 4096)
            cast(('w2', i), w2b[i], ffn_w2[i], 4096, 1024)

        prm_rows = uv(R, [1024], F32)
        rb33 = uv(R + 4096, [16], F32)
        rbrep = uv(R + 4096 + 1024, [128], F32)
        ohs = uv(R + 8192, [4, 512], F32)
        etab = uv(UBYTES - 4096, [4, 512], BF16)
        wsf = uv(R + 20480, [8, 128], F32)
        bsb = uv(R + 24576, [8, 128], F32)
        sinkb = uv(R + 28672, [32], F32)
        cfn = [
            lambda e: e.dma_start(out=ident[:], in_=ident_in),
            lambda e: e.dma_start(out=prm_rows[0:13, :], in_=prm_in),
            lambda e: e.dma_start(out=rb33[0:32, :], in_=rel_bias),
            lambda e: e.dma_start(out=ohs[0:33, :, :], in_=oh_in.rearrange("p (a b) -> p a b", a=4)),
            lambda e: e.dma_start(out=wsf[:, :, :], in_=b_ws.rearrange("g p q -> p g q")),
            lambda e: e.dma_start(out=bsb[:, :, :], in_=b_bs[0, :].partition_broadcast(128).rearrange("p (g q) -> p g q", g=8)),
            lambda e: e.dma_start(out=sinkb[:, :], in_=a_sink[0, :].partition_broadcast(128)),
        ]
        s.dma('sp', 'c0', cfn, reads=[], writes=['c_in'])
        for i in layers:
            cast_layer(i)
        if layers:
            cast_tick(12)
        if True:
            s.op('dve', lambda e: e.memset(ones[:], 1.0), writes=['ones'])
        if DBG >= 2:
            s.op('dve', lambda e: e.memset(rb33[32:33, :], -30000.0), reads=[], writes=['rb_m'])
            s.op('act', lambda e: e.activation(out=es[:], in_=sinkb[:, :], func=AF.Exp), reads=['c_in'], writes=['es'])
            pst, pk = s.ps()
            for c in range(8):
                s.op('pe', lambda e, c=c: e.transpose(out=pst[:, c * 16:c * 16 + 13], in_=prm_rows[0:13, c * 128:(c + 1) * 128], identity=ident[0:13, 0:13]),
                     reads=['c_in'], writes=[pk], signal=(c == 7))
            s.op('dve', lambda e: e.tensor_copy(out=colprm[:, :, 0:13], in_=pst[:, 0:128].rearrange("p (c k) -> p c k", c=8)[:, :, 0:13]),
                 reads=[pk], writes=['colprm'])
        if DBG >= 3:
            s.op('dve', lambda e: e.tensor_copy(out=rbrep[0:33, :].rearrange("p (h r) -> p h r", r=8), in_=rb33[0:33, :].unsqueeze(2).broadcast_to([33, 16, 8])),
                 reads=['c_in', 'rb_m'], writes=['rbrep'])
            for t in range(4):
                pt_, ptk = s.ps()
                s.op('pe', lambda e, t=t, pt_=pt_: e.matmul(pt_[:, :], lhsT=rbrep[0:33, :], rhs=ohs[0:33, t, :], start=True, stop=True),
                     reads=['c_in', 'rbrep'], writes=[ptk])
                s.op('act', lambda e, t=t, pt_=pt_: e.activation(out=etab[:, t, :], in_=pt_[:, :], func=AF.Exp), reads=[ptk], writes=[('etab', t)])
        if DBG >= 4:
            for t in range(4):
                tfn = [lambda e, t=t: e.dma_start(out=tabs[t].rearrange("h (r j) u -> (h r) j u", r=8), in_=etab[:, t, :].unsqueeze(1).broadcast_to([128, 16, 512]))]
                s.dma('sp', ('c1', t), tfn, reads=[('etab', t)], writes=[('tabs', t)])
        if DBG >= 5:
            for half in range(2):
                pw, pwk = s.ps()
                for j in range(4):
                    g = half * 4 + j
                    s.op('pe', lambda e, g=g, j=j, pw=pw: e.transpose(out=pw[:, j * 128:(j + 1) * 128], in_=wsf[:, g, :], identity=ident[:]),
                         reads=['c_in'], writes=[pwk], signal=(j == 3))
                s.op('act', lambda e, half=half, pw=pw: e.activation(out=WsT[:, half * 4:half * 4 + 4, :], in_=pw[:, :].rearrange("p (a b) -> p a b", a=4), func=AF.Copy),
                     reads=[pwk], writes=[('WsT', half)])
            for half in range(2):
                pr, prk = s.ps()
                for j in range(4):
                    g = half * 4 + j
                    s.op('pe', lambda e, g=g, j=j, pr=pr: e.matmul(pr[:, j * 128:(j + 1) * 128], lhsT=ones[:], rhs=WsT[:, g, :], start=True, stop=True),
                         reads=['ones', ('WsT', half)], writes=[prk], signal=(j == 3))
                for j in range(4):
                    g = half * 4 + j
                    for k in range(2):
                        fc = g * 2 + k
                        s.op('dve', lambda e, g=g, j=j, fc=fc, pr=pr: e.scalar_tensor_tensor(
                            out=Cterm[:, fc, :], in0=pr[:, j * 128:(j + 1) * 128], scalar=colprm[:, fc % 8, 11 + fc // 8:12 + fc // 8],
                            in1=bsb[:, g, :], op0=ALU.mult, op1=ALU.add), reads=[prk, 'colprm', 'c_in'], writes=[('Cterm', fc)])
        s.barrier(['c0'])

        def wload(parts, unit):
            cast_need(unit)
            slot = s.wsi % NSLOT
            s.wsi += 1
            sl = wslot[slot]
            fns = [(lambda e, o=ov(sl), d=dv: e.dma_start(out=o, in_=d)) for ov, dv in parts]
            s.dma('sp', ('wsem', slot), fns, reads=[('wb', unit)], writes=[('ws', slot)])
            return sl, ('ws', slot)

        def std_tile(w2d, r0, c0, ncols=512, nk=8):
            return [(lambda sl: sl[:, 0:nk, 0:ncols], w2d[r0:r0 + nk * 128, c0:c0 + ncols].rearrange("(k p) n -> p k n", p=128))]

        def stream(specs, compute, look=2):
            loaded = []
            n = len(specs)
            nxt = 0
            for i in range(n):
                while nxt < n and nxt <= i + look - 1:
                    loaded.append(wload(*specs[nxt]))
                    nxt += 1
                compute(i, *loaded[i])

        def rmsnorm_stats(rstd):
            sq = [uv(R + 8192, [T], BF16), uv(R + 12288, [T], BF16)]
            pss = [s.ps() for _ in range(4)]
            for c in range(8):
                b = sq[c % 2]
                s.op('act', lambda e, c=c, b=b: e.activation(out=b[:, :], in_=xT[:, c, :], func=AF.Square),
                     reads=[('xT', c, tb) for tb in range(4)], writes=[('sq', c % 2)])
                for tb in range(4):
                    p_, k_ = pss[tb]
                    s.mmg(p_[:, :], k_, [(ones[:], b[:, tb * 512:(tb + 1) * 512], [('sq', c % 2), 'ones'])], first=(c == 0), last=(c == 7))
            for tb in range(4):
                p_, k_ = pss[tb]
                s.op('act', lambda e, tb=tb, p_=p_: e.activation(out=rstd[:, tb * 512:(tb + 1) * 512], in_=p_[:, :], func=AF.Ln, bias=RMS_EPS, scale=1.0 / 1024),
                     reads=[k_], writes=[('rstd', tb)])
                s.op('act', lambda e, tb=tb: e.activation(out=rstd[:, tb * 512:(tb + 1) * 512], in_=rstd[:, tb * 512:(tb + 1) * 512], func=AF.Exp, scale=-0.5),
                     reads=[('rstd', tb)], writes=[('rstd', tb)])

        def rmsnorm(gi):
            rstd = uv(R, [T], F32)
            rmsnorm_stats(rstd)
            for c in range(8):
                eng = 'dve'
                s.op(eng, lambda e, c=c: e.scalar_tensor_tensor(out=hT[:, c, :], in0=xT[:, c, :], scalar=colprm[:, c, gi:gi + 1], in1=rstd[:, :],
                                                                 op0=ALU.mult, op1=ALU.mult),
                     reads=[('xT', c, tb) for tb in range(4)] + [('rstd', tb) for tb in range(4)] + ['colprm'], writes=[('hT', c)])
            s.barrier()

        def norm_sq(tb, sqbase):
            tsl = slice(tb * 512, (tb + 1) * 512)
            for c in range(8):
                bq = uv(sqbase + c * 1024, [512], BF16)
                s.op('act', lambda e, c=c, bq=bq: e.activation(out=bq[:, :], in_=xT[:, c, tsl], func=AF.Square), reads=[('xT', c, tb)], writes=[('nsq8', c)])

        def norm_tb(gi, tb, scr, final=False, sq8=None):
            rstd = uv(scr, [512], F32)
            sq = [uv(scr + 2048, [512], BF16), uv(scr + 3072, [512], BF16)]
            tsl = slice(tb * 512, (tb + 1) * 512)
            p_, k_ = s.ps()
            for c in range(8):
                if sq8 is not None:
                    bq = uv(sq8 + c * 1024, [512], BF16)
                    s.mmg(p_[:, :], k_, [(ones[:], bq[:, :], [('nsq8', c), 'ones'])], first=(c == 0), last=(c == 7))
                    continue
                bq = sq[c % 2]
                s.op('act', lambda e, c=c, bq=bq: e.activation(out=bq[:, :], in_=xT[:, c, tsl], func=AF.Square), reads=[('xT', c, tb)], writes=[('nsq', c % 2)])
                s.mmg(p_[:, :], k_, [(ones[:], bq[:, :], [('nsq', c % 2), 'ones'])], first=(c == 0), last=(c == 7))
            s.op('act', lambda e: e.activation(out=rstd[:, :], in_=p_[:, :], func=AF.Ln, bias=RMS_EPS, scale=1.0 / 1024), reads=[k_], writes=['nrstd'])
            s.op('act', lambda e: e.activation(out=rstd[:, :], in_=rstd[:, :], func=AF.Exp, scale=-0.5), reads=['nrstd'], writes=['nrstd'])
            for c in range(8):
                dst = xT if final else hT
                wk = ('xT', c, tb) if final else ('hT', c, tb)
                s.op('dve', lambda e, c=c, dst=dst: e.scalar_tensor_tensor(out=dst[:, c, tsl], in0=xT[:, c, tsl], scalar=colprm[:, c, gi:gi + 1], in1=rstd[:, :],
                                                                          op0=ALU.mult, op1=ALU.mult),
                     reads=[('xT', c, tb), 'nrstd', 'colprm'], writes=[wk])

        def store_tb(sq_, tb, ybase):
            yb = [uv(ybase + j * 4096, [1024], F32) for j in range(2)]
            for q4 in range(4):
                tt = tb * 4 + q4
                j = tt % 2
                for half in range(2):
                    p_, k_ = s.ps()
                    for q in range(4):
                        c = half * 4 + q
                        s.op('pe', lambda e, c=c, q=q, tt=tt, p_=p_: e.transpose(out=p_[:, q * 128:(q + 1) * 128], in_=xT[:, c, tt * 128:(tt + 1) * 128], identity=ident[:]),
                             reads=[('xT', c, tb)], writes=[k_], signal=(q == 3))
                    if half == 0:
                        s.op('act', lambda e, j=j, p_=p_: e.activation(out=yb[j][:, 0:512], in_=p_[:, :], func=AF.Copy), reads=[k_], writes=[('yout', j, 0)])
                    else:
                        s.op('dve', lambda e, j=j, p_=p_: e.tensor_copy(out=yb[j][:, 512:1024], in_=p_[:, :]), reads=[k_], writes=[('yout', j, 1)])
                s.dma('sp', ('yst', j), [lambda e, tt=tt, j=j: e.dma_start(out=y[sq_, tt * 128:(tt + 1) * 128, :], in_=yb[j][:, :])],
                      reads=[('yout', j, 0), ('yout', j, 1)], writes=[], union=True)

        def load_dma_tb(sq_, tb, xbase):
            xb = [uv(xbase + q * 4096, [1024], F32) for q in range(4)]
            for q4 in range(4):
                tt = tb * 4 + q4
                s.dma('sp', ('xin2', q4), [lambda e, tt=tt, q4=q4: e.dma_start(out=xb[q4][:, :], in_=x[sq_, tt * 128:(tt + 1) * 128, :])],
                      reads=[], writes=[('xin2', q4)], union=True)

        def load_tr_tb(tb, xbase):
            xb = [uv(xbase + q * 4096, [1024], F32) for q in range(4)]
            for q4 in range(4):
                tt = tb * 4 + q4
                for half in range(2):
                    p_, k_ = s.ps()
                    for q in range(4):
                        c = half * 4 + q
                        s.op('pe', lambda e, q4=q4, c=c, q=q, p_=p_: e.transpose(out=p_[:, q * 128:(q + 1) * 128], in_=xb[q4][:, c * 128:(c + 1) * 128], identity=ident[:]),
                             reads=[('xin2', q4)], writes=[k_], signal=(q == 3))
                    wk = [('xT', half * 4 + q, tb) for q in range(4)]
                    if half == 0:
                        s.op('act', lambda e, half=half, tt=tt, p_=p_: e.activation(out=xT[:, half * 4:half * 4 + 4, tt * 128:(tt + 1) * 128],
                                                                              in_=p_[:, :].rearrange("p (a b) -> p a b", a=4), func=AF.Copy), reads=[k_], writes=wk)
                    else:
                        s.op('dve', lambda e, half=half, tt=tt, p_=p_: e.tensor_copy(out=xT[:, half * 4:half * 4 + 4, tt * 128:(tt + 1) * 128],
                                                                               in_=p_[:, :].rearrange("p (a b) -> p a b", a=4)), reads=[k_], writes=wk)

        class Nxt:
            def __init__(self, kind, arg, nload=None):
                self.kind, self.arg, self.nload = kind, arg, nload
                self.sq = {}

            def pre(self, tb, xbase, sqbase=None):
                if sqbase is not None:
                    norm_sq(tb, sqbase)
                    self.sq[tb] = sqbase
                if self.kind == 'final' and self.nload is not None:
                    load_dma_tb(self.nload[0], tb, xbase)

            def emit(self, tb, scr, ybase, xbase=None):
                if self.kind == 'norm':
                    norm_tb(self.arg, tb, scr, sq8=self.sq.get(tb))
                else:
                    norm_tb(8, tb, scr, final=True, sq8=self.sq.get(tb))
                    store_tb(self.arg, tb, ybase)
                    if self.nload is not None and xbase is not None:
                        load_tr_tb(tb, xbase)
                        norm_tb(self.nload[1], tb, scr)

        def resid_add(m, tb, p_, k_):
            s.op('dve', lambda e: e.tensor_tensor(out=xT[:, m, tb * 512:(tb + 1) * 512], in0=xT[:, m, tb * 512:(tb + 1) * 512], in1=p_[:, :], op=ALU.add),
                 reads=[k_, ('xT', m, tb)], writes=[('xT', m, tb)])

        def ffn(i, nxt):
            scr, ybase, xbase = R + 40960, R + 45056, R + 53248
            aT = uv(R, [32, 512], BF16)
            rtmp = [uv(R + 32768 + j * 2048, [512], F32) for j in range(4)]
            rcnt = [0]
            for tb in range(4):
                tsl = slice(tb * 512, (tb + 1) * 512)

                def c1(cb, sl, sk):
                    for m in range(4):
                        p_, k_ = s.ps()
                        s.mmg(p_[:, :], k_, [(sl[:, kc, m * 128:(m + 1) * 128], hT[:, kc, tsl], [sk, ('hT', kc, tb)]) for kc in range(8)])
                        j = rcnt[0] % 4
                        rcnt[0] += 1
                        rt = rtmp[j]
                        fc = cb * 4 + m
                        s.op('act', lambda e, rt=rt, p_=p_: e.activation(out=rt[:, :], in_=p_[:, :], func=AF.Relu), reads=[k_], writes=[('rt', j)])
                        s.op('pool', lambda e, rt=rt, fc=fc: e.tensor_tensor(out=aT[:, fc, :], in0=rt[:, :], in1=rt[:, :], op=ALU.mult),
                             reads=[('rt', j)], writes=[('aT', fc)])
                if tb >= 1:
                    nxt.pre(tb - 1, xbase, R + 69632)
                stream([(std_tile(w1b[i], 0, cb * 512), ('w1', i)) for cb in range(8)], c1)
                if tb >= 1:
                    nxt.emit(tb - 1, scr, ybase, xbase)
                for cb in range(2):
                    pss = [s.ps() for _ in range(4)]

                    def c2(kg, sl, sk, cb=cb, pss=pss):
                        for m in range(4):
                            p_, k_ = pss[m]
                            s.mmg(p_[:, :], k_, [(sl[:, kc, m * 128:(m + 1) * 128], aT[:, kg * 8 + kc, :], [sk, ('aT', kg * 8 + kc)]) for kc in range(8)],
                                  first=(kg == 0), last=(kg == 3))
                            if kg == 3:
                                resid_add(cb * 4 + m, tb, p_, k_)
                    stream([(std_tile(w2b[i], kg * 1024, cb * 512), ('w2', i)) for kg in range(4)], c2)
            nxt.pre(3, xbase, R + 69632)
            nxt.emit(3, scr, ybase, xbase)
            s.barrier([('yst', 0), ('yst', 1)] + [('xin2', q) for q in range(4)])

        def attn_bufs(base):
            b = {}
            b['qT'] = uv(base, [2, T], BF16)
            b['kT'] = uv(base + 8192, [T], BF16)
            b['V'] = uv(base + 12288, [16, 192], BF16)
            b['EB'] = uv(base + 18432, [3, 512], BF16)
            b['pT'] = [uv(base + 21504 + j * 1024, [512], BF16) for j in range(6)]
            b['e'] = [uv(base + 27648 + j * 1024, [512], BF16) for j in range(3)]
            b['oT'] = uv(base + 30720, [2, T], BF16)
            b['tmp'] = uv(base + 38912, [256], F32)
            b['rec'] = uv(base + 39936, [256], F32)
            b['kTo'] = uv(base + 40960, [T], BF16)
            b['end'] = base + 45056
            return b

        def project_qkv(b, wb2d, unit, qc0, kc0, vc0, perm):
            parts = [(lambda sl: sl[:, :, 0:256], wb2d[:, qc0:qc0 + 256].rearrange("(k p) n -> p k n", p=128)),
                     (lambda sl: sl[:, :, 256:320], wb2d[:, kc0:kc0 + 64].rearrange("(k p) n -> p k n", p=128)),
                     (lambda sl: sl[:, :, 320:384], wb2d[:, kc0:kc0 + 64].rearrange("(k p) n -> p k n", p=128)),
                     (lambda sl: sl[:, :, 384:448], wb2d[:, vc0:vc0 + 64].rearrange("(k p) n -> p k n", p=128))]
            sl, sk = wload(parts, unit)
            tg = b.get('tag', '')
            D = perm
            L = T // D
            cnt = 0
            for mi in range(3):
                for tb in range(4):
                    p_, k_ = s.ps()
                    s.mmg(p_[:, :], k_, [(sl[:, kc, mi * 128:(mi + 1) * 128], hT[:, kc, tb * 512:(tb + 1) * 512], [sk, ('hT', kc, tb)]) for kc in range(8)])
                    if mi < 2:
                        jobs = [(b['qT'][:, mi, :], p_[:, :], (tg + 'qT', mi, tb), 0.125)]
                    else:
                        jobs = [(b['kT'][0:64, :], p_[0:64, :], (tg + 'kT', tb), 1.0), (b['kTo'][64:128, :], p_[64:128, :], (tg + 'kTo', tb), 1.0)]
                    for dst, src, wkey, sc in jobs:
                        if D == 1:
                            o_ap = dst[:, tb * 512:(tb + 1) * 512]
                            i_ap = src
                        else:
                            n_i = 512 // D
                            o_ap = dst.rearrange("p (r i) -> p r i", r=D)[:, :, tb * n_i:(tb + 1) * n_i]
                            i_ap = src.rearrange("p (i r) -> p r i", r=D)
                        if cnt % 2 == 0:
                            s.op('act', lambda e, o_ap=o_ap, i_ap=i_ap, sc=sc: e.activation(out=o_ap, in_=i_ap, func=AF.Copy, scale=sc), reads=[k_], writes=[wkey])
                        else:
                            s.op('dve', lambda e, o_ap=o_ap, i_ap=i_ap, sc=sc: e.tensor_scalar(out=o_ap, in0=i_ap, scalar1=sc, scalar2=None, op0=ALU.mult), reads=[k_], writes=[wkey])
                        cnt += 1
                    yield
            for half in range(2):
                p_, k_ = s.ps()
                for j in range(8):
                    tt = half * 8 + j
                    r, idx0 = (tt * 128) // L, (tt * 128) % L
                    t0 = idx0 * D + r
                    s.mmg(p_[:, j * 64:(j + 1) * 64], k_,
                          [(hT[:, kc, t0:t0 + 127 * D + 1:D], sl[:, kc, 384:448], [sk] + ([('hT', kc, t0 // 512)] if D == 1 else [('hT', kc, q_) for q_ in range(4)])) for kc in range(8)])
                s.op('act', lambda e, half=half, p_=p_: e.activation(out=b['V'][:, half * 8:half * 8 + 8, 64:128], in_=p_[:, :].rearrange("p (a b) -> p a b", a=8), func=AF.Copy),
                     reads=[k_], writes=[(tg + 'V', half)])
                yield

        def attn_core(b, nbs, pv_evac):
            qT, kT, V, EB = b['qT'], b['kT'], b['V'], b['EB']
            tg = b.get('tag', '')
            pcnt = [0]
            ecnt = [0]

            def s_stage(qb):
                outp = []
                for bt in range(3):
                    kb = qb + bt - 1
                    if kb // nbs != qb // nbs or kb < 0 or kb >= 16:
                        continue
                    p_, k_ = s.ps()
                    tq = qb // 4
                    tk = kb // 4
                    qs = slice(qb * 128, (qb + 1) * 128)
                    ks = slice(kb * 128, (kb + 1) * 128)
                    s.mmg(p_[:, 0:256].rearrange("p (a b) -> p a b", a=2), k_, [(kT[:, ks], qT[:, :, qs], [(tg + 'kT', tk), tg + 'kzero', (tg + 'qT', 0, tq), (tg + 'qT', 1, tq)])])
                    s.mmg(p_[:, 256:512].rearrange("p (a b) -> p a b", a=2), k_, [(b['kTo'][:, ks], qT[:, :, qs], [(tg + 'kTo', tk), tg + 'kzero', (tg + 'qT', 0, tq), (tg + 'qT', 1, tq)])])
                    je = ecnt[0] % 3
                    ecnt[0] += 1
                    jp = pcnt[0] % 6
                    pcnt[0] += 1
                    eb_, pT_ = b['e'][je], b['pT'][jp]
                    s.op('act', lambda e, eb_=eb_, p_=p_: e.activation(out=eb_[:, :], in_=p_[:, :], func=AF.Exp), reads=[k_], writes=[('e', je)])
                    eng = 'dve' if (pcnt[0] % 2 == 0) else 'pool'
                    if DBG >= 13.2:
                        s.op(eng, lambda e, eb_=eb_, pT_=pT_, bt=bt: e.tensor_tensor(out=pT_[:, :], in0=eb_[:, :], in1=EB[:, bt, :], op=ALU.mult),
                             reads=[('e', je), 'EB'], writes=[('pT', jp)])
                    outp.append((kb, pT_, ('pT', jp)))
                return outp

            def pv_stage(qb, plist):
                if DBG < 13.4:
                    return
                pa, ka = s.ps()
                pb, kb_ = s.ps()
                n = len(plist)
                for i, (kb, pT_, pk) in enumerate(plist):
                    s.mmg(pa[:, 0:256], ka, [(V[:, kb, 64:192], pT_[:, 0:256], [pk, (tg + 'V', kb // 8), tg + 'Vones'])], first=(i == 0), last=(i == n - 1))
                    s.mmg(pb[:, 0:256], kb_, [(V[:, kb, 0:128], pT_[:, 256:512], [pk, (tg + 'V', kb // 8), tg + 'Vones'])], first=(i == 0), last=(i == n - 1))
                if DBG >= 13.6:
                    pv_evac(qb, pa, ka, pb, kb_)

            prev = s_stage(0)
            for qb in range(16):
                nxt = s_stage(qb + 1) if qb + 1 < 16 else None
                pv_stage(qb, prev)
                prev = nxt
                cast_tick(2)
                yield

        def wo_partial(b, wob2d, unit, g, after_tb=None):
            oT = b['oT']
            tg = b.get('tag', '')
            cast_need(unit)
            slot = s.wsi % NSLOT
            s.wsi += 1
            sl = wslot[slot]
            slv = sl[:, 0:4, :].rearrange("p a b -> p (a b)").rearrange("p (k n) -> p k n", k=2)
            s.dma('sp', ('wsem', slot), [lambda e: e.dma_start(out=slv, in_=wob2d[g * 256:(g + 1) * 256, :].rearrange("(k p) n -> p k n", p=128))],
                  reads=[('wb', unit)], writes=[('ws', slot)])
            sk = ('ws', slot)
            for tb in range(4):
                for m in range(8):
                    p_, k_ = s.ps()
                    s.mmg(p_[:, :], k_, [(slv[:, kc, m * 128:(m + 1) * 128], oT[:, kc, tb * 512:(tb + 1) * 512], [sk, (tg + 'oT', kc, tb)]) for kc in range(2)])
                    resid_add(m, tb, p_, k_)
                    yield
                if after_tb is not None and tb >= 1:
                    after_tb(tb - 1)
            if after_tb is not None:
                after_tb(3)

        def load_EB(b, t, g):
            fns = []
            for bt in range(3):
                o_bt = 255 - (bt - 1) * 128
                for par in range(2):
                    src = bass.AP(tabs_t, t * 16 * 128 * 512 + (4 * g + par) * 128 * 512 + o_bt, [[511, 128], [2 * 128 * 512, 2], [1, 128]])
                    fns.append(lambda e, bt=bt, par=par, src=src: e.dma_start(
                        out=b['EB'][:, bt, par * 256:(par + 1) * 256].rearrange("p (c i) -> p c i", c=2), in_=src))
            s.dma('sp', 'ebsem', fns, reads=[('tabs', t)], writes=['EB'], union=True)

        def v_ones(b):
            tg = b.get('tag', '')
            s.op('pool', lambda e: e.memset(b['V'][:, :, 0:64], 1.0), writes=[tg + 'Vones'])
            s.op('pool', lambda e: e.memset(b['V'][:, :, 128:192], 1.0), writes=[tg + 'Vones'])
            s.op('pool', lambda e: e.memset(b['kT'][64:128, :], 0.0), writes=[tg + 'kzero'])
            s.op('pool', lambda e: e.memset(b['kTo'][0:64, :], 0.0), writes=[tg + 'kzero'])

        def mixer_a(i, j):
            s.barrier([('c1', t) for t in range(4)])
            SB = R + 45056
            shared = {'EB': uv(SB, [3, 512], BF16),
                      'pT': [uv(SB + 3072 + q * 1024, [512], BF16) for q in range(6)],
                      'e': [uv(SB + 9216 + q * 1024, [512], BF16) for q in range(3)]}
            lnd, rec = uv(SB + 12288, [256], F32), uv(SB + 13312, [256], F32)
            sets = []
            for st in range(2):
                base = R + st * 22528
                d = dict(shared)
                d.update({'tag': 'A%d' % st, 'qT': uv(base, [2, T], BF16), 'kT': uv(base + 8192, [T], BF16), 'kTo': uv(base + 12288, [T], BF16),
                          'V': uv(base + 16384, [16, 192], BF16), 'oT': uv(SB + 14336 + st * 8192, [2, T], BF16)})
                sets.append(d)
                v_ones(d)

            def proj(g):
                return project_qkv(sets[g % 2], aqkvb[j], ('aqkv', j), g * 256, 1024 + g * 64, 1280 + g * 64, 1)

            def mk_evac(g, b):
                oT, tg = b['oT'], b['tag']

                def pv_evac(qb, pa, ka, pb, kb_):
                    tq = qb // 4
                    qs = slice(qb * 128, (qb + 1) * 128)
                    okeys = [(tg + 'oT', 0, tq), (tg + 'oT', 1, tq)]
                    for mq in range(2):
                        ha = j * 16 + 4 * g + 2 * mq
                        s.op('act', lambda e, mq=mq, ha=ha: e.activation(out=lnd[64:128, mq * 128:(mq + 1) * 128], in_=pa[64:128, mq * 128:(mq + 1) * 128],
                                                                         func=AF.Ln, bias=es[64:128, ha:ha + 1]), reads=[ka, 'es'], writes=['lnda'])
                    s.op('act', lambda e: e.activation(out=rec[0:64, :], in_=lnd[64:128, :], func=AF.Exp, scale=-1.0), reads=['lnda'], writes=['reca'])
                    s.op('dve', lambda e: e.tensor_tensor(out=oT[0:64, :, qs], in0=pa[0:64, 0:256].rearrange("p (a b) -> p a b", a=2),
                                                          in1=rec[0:64, :].rearrange("p (a b) -> p a b", a=2), op=ALU.mult),
                         reads=[ka, 'reca'], writes=okeys)
                    for mq in range(2):
                        hb = j * 16 + 4 * g + 2 * mq + 1
                        s.op('act', lambda e, mq=mq, hb=hb: e.activation(out=lnd[0:64, mq * 128:(mq + 1) * 128], in_=pb[0:64, mq * 128:(mq + 1) * 128],
                                                                         func=AF.Ln, bias=es[0:64, hb:hb + 1]), reads=[kb_, 'es'], writes=['lndb'])
                    s.op('act', lambda e: e.activation(out=rec[64:128, :], in_=lnd[0:64, :], func=AF.Exp, scale=-1.0), reads=['lndb'], writes=['recb'])
                    s.op('dve', lambda e: e.tensor_tensor(out=oT[64:128, :, qs], in0=pb[64:128, 0:256].rearrange("p (a b) -> p a b", a=2),
                                                          in1=rec[64:128, :].rearrange("p (a b) -> p a b", a=2), op=ALU.mult),
                         reads=[kb_, 'recb'], writes=okeys)
                return pv_evac

            for _ in proj(0):
                pass
            pending = None
            for g in range(4):
                b = sets[g % 2]
                load_EB(b, 0, g)
                side = []
                if g + 1 < 4:
                    side.append((proj(g + 1), 1))
                if pending is not None:
                    side.append((pending, 2))
                if NOILV:
                    for sg, n in side:
                        for _q in sg:
                            pass
                    side = []
                for _ in attn_core(b, 16, mk_evac(g, b)):
                    for sg, n in list(side):
                        for _q in range(n):
                            try:
                                next(sg)
                            except StopIteration:
                                side = [x for x in side if x[0] is not sg]
                                break
                for sg, n in side:
                    for _q in sg:
                        pass
                if g < 3:
                    pending = wo_partial(b, awob[j], ('awo', j), g)
            s.barrier(['ebsem'])
            nf = Nxt('norm', 4 + i)
            for _ in wo_partial(sets[1], awob[j], ('awo', j), 3, after_tb=lambda tb_: nf.emit(tb_, R, 0)):
                pass
            s.barrier(['ebsem'])

        def mixer_c(i):
            s.barrier([('c1', t) for t in range(4)])
            b = attn_bufs(R)
            acc_a = uv(b['end'], [2, T], F32)
            acc_b = uv(b['end'] + 16384, [2, T], F32)
            reca = uv(R + 38912, [2, 256], F32)
            v_ones(b)
            oT = b['oT']
            def proj(g_, dg_):
                base = dg_ * 1536
                for _ in project_qkv(b, cqkvb[0], ('cqkv', 0), base + g_ * 256, base + 1024 + g_ * 64, base + 1280 + g_ * 64, (1, 4, 16)[dg_]):
                    pass

            proj(0, 0)
            for g in range(4):
                for dg, D in enumerate((1, 4, 16)):
                    L = T // D
                    load_EB(b, 1 + dg, g)

                    def pv_evac(qb, pa, ka, pb, kb_, dg=dg, D=D, L=L):
                        r, idx0 = (qb * 128) // L, (qb * 128) % L
                        t0 = idx0 * D + r
                        tsl = slice(t0, t0 + 127 * D + 1, D)
                        for (pp, kk, acc, nm) in ((pa, ka, acc_a, 'acca'), (pb, kb_, acc_b, 'accb')):
                            src = pp[:, 0:256].rearrange("p (a b) -> p a b", a=2)
                            if dg == 0:
                                s.op('dve', lambda e, acc=acc, src=src: e.tensor_copy(out=acc[:, :, tsl], in_=src), reads=[kk], writes=[nm])
                            else:
                                s.op('dve', lambda e, acc=acc, src=src: e.tensor_tensor(out=acc[:, :, tsl], in0=src, in1=acc[:, :, tsl], op=ALU.add),
                                     reads=[kk, nm], writes=[nm])
                    for _ in attn_core(b, L // 128, pv_evac):
                        pass
                    if dg < 2:
                        proj(g, dg + 1)
                    elif g < 3:
                        proj(g + 1, 0)
                for tb8 in range(8):
                    tb = tb8 // 2
                    ts_ = slice(tb8 * 256, (tb8 + 1) * 256)
                    s.op('act', lambda e, ts_=ts_: e.activation(out=acc_a[64:128, :, ts_], in_=acc_a[64:128, :, ts_], func=AF.Ln), reads=['acca'], writes=['acca'])
                    s.op('act', lambda e, ts_=ts_: e.activation(out=reca[0:64, :, :], in_=acc_a[64:128, :, ts_], func=AF.Exp, scale=-1.0), reads=['acca'], writes=['reca'])
                    s.op('pool', lambda e, ts_=ts_: e.tensor_tensor(out=oT[0:64, :, ts_], in0=acc_a[0:64, :, ts_], in1=reca[0:64, :, :], op=ALU.mult),
                         reads=['acca', 'reca'], writes=[('oT', 0, tb), ('oT', 1, tb)])
                    s.op('act', lambda e, ts_=ts_: e.activation(out=acc_b[0:64, :, ts_], in_=acc_b[0:64, :, ts_], func=AF.Ln), reads=['accb'], writes=['accb'])
                    s.op('act', lambda e, ts_=ts_: e.activation(out=reca[64:128, :, :], in_=acc_b[0:64, :, ts_], func=AF.Exp, scale=-1.0), reads=['accb'], writes=['recb'])
                    s.op('pool', lambda e, ts_=ts_: e.tensor_tensor(out=oT[64:128, :, ts_], in0=acc_b[64:128, :, ts_], in1=reca[64:128, :, :], op=ALU.mult),
                         reads=['accb', 'recb'], writes=[('oT', 0, tb), ('oT', 1, tb)])
                if g < 3:
                    for _ in wo_partial(b, cwob[0], ('cwo', 0), g):
                        pass
                else:
                    s.barrier(['ebsem'])
                    nf = Nxt('norm', 4 + i)
                    for _ in wo_partial(b, cwob[0], ('cwo', 0), g, after_tb=lambda tb_: nf.emit(tb_, R, 0)):
                        pass
            s.barrier(['ebsem'])

        def mixer_b(i):
            nf = Nxt('norm', 4 + i)
            bscr = R + 70656
            uT = uv(R, [16, 512], BF16)
            vtok = uv(R + 16384, [4, 2048], F32)
            vn = uv(R + 49152, [4, 2048], BF16)
            stats = uv(R + 65536, [4, 24], F32)
            mv = uv(R + 65536 + 512, [4, 2], F32)
            ttmp = [uv(R + 66560 + j * 2048, [512], F32) for j in range(2)]
            wb2d = bwinb[0]
            tcnt = [0]

            def do_cv(tb):
                def cv(cb, sl, sk):
                    for tc in range(4):
                        p_, k_ = s.ps()
                        t0 = tb * 512 + tc * 128
                        s.mmg(p_[:, :], k_, [(hT[:, kc, t0:t0 + 128], sl[:, kc, :], [sk, ('hT', kc, tb)]) for kc in range(8)])
                        s.op('act', lambda e, tc=tc, cb=cb, p_=p_: e.activation(out=vtok[:, tc, cb * 512:(cb + 1) * 512], in_=p_[:, :], func=AF.Gelu),
                             reads=[k_], writes=[('vtok', tc, cb)])
                        s.op('dve', lambda e, tc=tc, cb=cb: e.bn_stats(out=stats[:, tc, cb * 6:(cb + 1) * 6], in_=vtok[:, tc, cb * 512:(cb + 1) * 512]),
                             reads=[('vtok', tc, cb)], writes=[('stats', tc, cb)])
                stream([(std_tile(wb2d, 0, 2048 + cb * 512), ('bwin', 0)) for cb in range(4)], cv)

            def do_cu(tb):
                tsl = slice(tb * 512, (tb + 1) * 512)

                def cu(cb, sl, sk):
                    for m in range(4):
                        p_, k_ = s.ps()
                        s.mmg(p_[:, :], k_, [(sl[:, kc, m * 128:(m + 1) * 128], hT[:, kc, tsl], [sk, ('hT', kc, tb)]) for kc in range(8)])
                        fc = cb * 4 + m
                        s.op('act', lambda e, fc=fc, p_=p_: e.activation(out=uT[:, fc, :], in_=p_[:, :], func=AF.Gelu), reads=[k_], writes=[('uT', fc)])
                stream([(std_tile(wb2d, 0, cb * 512), ('bwin', 0)) for cb in range(4)], cu)

            def do_ln(tb):
                for tc in range(4):
                    s.op('dve', lambda e, tc=tc: e.bn_aggr(out=mv[:, tc, :], in_=stats[:, tc, :]), reads=[('stats', tc, cb) for cb in range(4)], writes=[('mv', tc)])
                    s.op('act', lambda e, tc=tc: e.activation(out=mv[:, tc, 1:2], in_=mv[:, tc, 1:2], func=AF.Sqrt, bias=LN_EPS, scale=1.0), reads=[('mv', tc)], writes=[('mv', tc)])
                    s.op('dve', lambda e, tc=tc: e.reciprocal(out=mv[:, tc, 1:2], in_=mv[:, tc, 1:2]), reads=[('mv', tc)], writes=[('mv', tc)])
                    s.op('dve', lambda e, tc=tc: e.tensor_scalar(out=vn[:, tc, :], in0=vtok[:, tc, :], scalar1=mv[:, tc, 0:1], scalar2=mv[:, tc, 1:2],
                                                                   op0=ALU.subtract, op1=ALU.mult),
                         reads=[('mv', tc)] + [('vtok', tc, cb) for cb in range(4)], writes=[('vn', tc)])

            def do_spatial(tb):
                for fc in range(16):
                    g = fc // 2
                    p_, k_ = s.ps()
                    for tc in range(4):
                        s.mmg(p_[:, tc * 128:(tc + 1) * 128], k_, [(vn[:, tc, fc * 128:(fc + 1) * 128], WsT[:, g, :], [('vn', tc), ('WsT', g // 4)])])
                    j = tcnt[0] % 2
                    tcnt[0] += 1
                    tt_ = ttmp[j]
                    for tc in range(4):
                        s.op('dve', lambda e, tc=tc, fc=fc, tt_=tt_, p_=p_: e.scalar_tensor_tensor(
                            out=tt_[:, tc * 128:(tc + 1) * 128], in0=p_[:, tc * 128:(tc + 1) * 128], scalar=colprm[:, fc % 8, 9 + fc // 8:10 + fc // 8],
                            in1=Cterm[:, fc, :], op0=ALU.mult, op1=ALU.add), reads=[k_, ('Cterm', fc), 'colprm'], writes=[('tt', j)])
                    s.op('pool', lambda e, fc=fc, tt_=tt_: e.tensor_tensor(out=uT[:, fc, :], in0=tt_[:, :], in1=uT[:, fc, :], op=ALU.mult),
                         reads=[('tt', j), ('uT', fc)], writes=[('uT', fc)])

            def do_out(tb):
                for cb in range(2):
                    pss = [s.ps() for _ in range(4)]

                    def co(kg, sl, sk, cb=cb, pss=pss):
                        for m in range(4):
                            p_, k_ = pss[m]
                            s.mmg(p_[:, :], k_, [(sl[:, kc, m * 128:(m + 1) * 128], uT[:, kg * 8 + kc, :], [sk, ('uT', kg * 8 + kc)]) for kc in range(8)],
                                  first=(kg == 0), last=(kg == 1))
                            if kg == 1:
                                resid_add(cb * 4 + m, tb, p_, k_)
                    stream([(std_tile(bwob[0], kg * 1024, cb * 512), ('bwo', 0)) for kg in range(2)], co)

            do_cv(0)
            do_ln(0)
            do_cu(0)
            do_spatial(0)
            for tb in range(4):
                if tb + 1 < 4:
                    do_cv(tb + 1)
                    do_ln(tb + 1)
                do_out(tb)
                if tb + 1 < 4:
                    do_cu(tb + 1)
                    do_spatial(tb + 1)
                nf.emit(tb, bscr, 0)
            s.barrier()

        xin = [uv(R + 16384 + j * 4096, [1024], F32) for j in range(2)]
        yout = [uv(R + 24576 + j * 4096, [1024], F32) for j in range(2)]

        def load_x(sq_, nxt):
            for tt in range(16):
                j = tt % 2
                s.dma('sp', ('xin', j), [lambda e, tt=tt, j=j: e.dma_start(out=xin[j][:, :], in_=x[sq_, tt * 128:(tt + 1) * 128, :])], reads=[], writes=[('xin', j)], union=True)
                for half in range(2):
                    p_, k_ = s.ps()
                    for q in range(4):
                        c = half * 4 + q
                        s.op('pe', lambda e, j=j, c=c, q=q, p_=p_: e.transpose(out=p_[:, q * 128:(q + 1) * 128], in_=xin[j][:, c * 128:(c + 1) * 128], identity=ident[:]),
                             reads=[('xin', j)], writes=[k_], signal=(q == 3))
                    eng = 'act' if half == 0 else 'dve'
                    if eng == 'act':
                        s.op('act', lambda e, half=half, tt=tt, p_=p_: e.activation(out=xT[:, half * 4:half * 4 + 4, tt * 128:(tt + 1) * 128],
                                                                              in_=p_[:, :].rearrange("p (a b) -> p a b", a=4), func=AF.Copy),
                             reads=[k_], writes=[('xT', half * 4 + q, tt // 4) for q in range(4)])
                    else:
                        s.op('dve', lambda e, half=half, tt=tt, p_=p_: e.tensor_copy(out=xT[:, half * 4:half * 4 + 4, tt * 128:(tt + 1) * 128],
                                                                               in_=p_[:, :].rearrange("p (a b) -> p a b", a=4)),
                             reads=[k_], writes=[('xT', half * 4 + q, tt // 4) for q in range(4)])
                if tt % 4 == 3 and tt // 4 >= 1:
                    nxt.emit(tt // 4 - 1, R, R + 4096)
            nxt.emit(3, R, R + 4096)
            s.barrier([('xin', 0), ('xin', 1), ('yst', 0), ('yst', 1)])

        def store_y(sq_):
            rstd = uv(R, [T], F32)
            rmsnorm_stats(rstd)
            for c in range(8 if DBG >= 8 else 0):
                eng = 'dve'
                s.op(eng, lambda e, c=c: e.scalar_tensor_tensor(out=xT[:, c, :], in0=xT[:, c, :], scalar=colprm[:, c, 8:9], in1=rstd[:, :], op0=ALU.mult, op1=ALU.mult),
                     reads=[('rstd', tb) for tb in range(4)] + ['colprm'] + [('xT', c, tb) for tb in range(4)], writes=[('xT', c, tb) for tb in range(4)])
            for tt in range(16 if DBG >= 9 else 0):
                j = tt % 2
                for half in range(2):
                    p_, k_ = s.ps()
                    for q in range(4):
                        c = half * 4 + q
                        s.op('pe', lambda e, c=c, q=q, tt=tt, p_=p_: e.transpose(out=p_[:, q * 128:(q + 1) * 128], in_=xT[:, c, tt * 128:(tt + 1) * 128], identity=ident[:]),
                             reads=[('xT', c, tt // 4)], writes=[k_], signal=(q == 3))
                    if half == 0:
                        s.op('act', lambda e, j=j, p_=p_: e.activation(out=yout[j][:, 0:512], in_=p_[:, :], func=AF.Copy), reads=[k_], writes=[('yout', j, 0)])
                    else:
                        s.op('dve', lambda e, j=j, p_=p_: e.tensor_copy(out=yout[j][:, 512:1024], in_=p_[:, :]), reads=[k_], writes=[('yout', j, 1)])
                if DBG >= 10:
                  s.dma('sp', ('yst', j), [lambda e, tt=tt, j=j: e.dma_start(out=y[sq_, tt * 128:(tt + 1) * 128, :], in_=yout[j][:, :])],
                      reads=[('yout', j, 0), ('yout', j, 1)], writes=[], union=True)
            s.barrier([('yst', 0), ('yst', 1)])

        for sq_ in range(nseq if DBG >= 6 else 0):
            if sq_ == 0 or not layers:
                load_x(sq_, Nxt('norm', layers[0]) if layers else Nxt('final', sq_))
            for li, i in enumerate(layers):
                kind, j = i % 3, i // 3
                if SKIPMIX:
                    for tb in range(4):
                        norm_tb(4 + i, tb, R)
                    s.barrier()
                elif kind == 0:
                    mixer_a(i, j)
                elif kind == 1:
                    mixer_b(i)
                else:
                    mixer_c(i)
                if li + 1 < len(layers):
                    ffn(i, Nxt('norm', layers[li + 1]))
                else:
                    ffn(i, Nxt('final', sq_, nload=((sq_ + 1, layers[0]) if sq_ + 1 < nseq else None)))
        s.final_wait('pool', [('yst', 0), ('yst', 1)])
        s.final_wait('sp', [('yst', 0), ('yst', 1)])
        s.emit(block)
    return nc


_OH = None


def _consts():
    global _OH
    if _OH is None:
        _OH = (np.eye(128, dtype=np.float32), np.ascontiguousarray(_onehot_tables().reshape(33, 4 * 512)))
    return _OH


def make_in_maps(inputs, seq_lists):
    ident, oh = _consts()
    f = lambda a: np.ascontiguousarray(np.asarray(a, dtype=np.float32))
    prm = np.concatenate([f(inputs["norm_mix_g"]), f(inputs["norm_ffn_g"]), f(inputs["final_g"]).reshape(1, 1024),
                          f(inputs["b_ln_g"]).reshape(2, 1024), f(inputs["b_ln_b"]).reshape(2, 1024)], axis=0)
    shared = {
        "rel_bias": f(inputs["rel_bias"]), "prm": np.ascontiguousarray(prm),
        "ffn_w1": f(inputs["ffn_w1"]), "ffn_w2": f(inputs["ffn_w2"]),
        "a_wqkv": f(inputs["a_wqkv"]), "a_sink": f(inputs["a_sink"]).reshape(1, 32), "a_wo": f(inputs["a_wo"]),
        "b_win": f(inputs["b_win"]), "b_ws": f(inputs["b_ws"]).reshape(8, 128, 128), "b_bs": f(inputs["b_bs"]).reshape(1, 1024),
        "b_wo": f(inputs["b_wo"]), "c_wqkv": f(inputs["c_wqkv"]), "c_wo": f(inputs["c_wo"]),
        "ident": ident, "oh": oh,
    }
    xp, xs = inputs["x_prompt"], inputs["x_sample"]
    maps = []
    for sl in seq_lists:
        xs_ = np.stack([np.asarray(xp[b] if grp == 0 else xs[b], dtype=np.float32) for grp, b in sl], axis=0)
        m = dict(shared)
        m["x"] = np.ascontiguousarray(xs_)
        maps.append(m)
    return maps


def kernel(**inputs):
    seq_lists = [[(0, 2 * c), (0, 2 * c + 1), (1, c)] for c in range(NCORES)]
    nc = build(3)
    maps = make_in_maps(inputs, seq_lists)
    res = run_bass_kernel_spmd(nc, maps, core_ids=list(range(NCORES)))
    yp = np.empty((16, T, 1024), np.float32)
    ys = np.empty((8, T, 1024), np.float32)
    for c in range(NCORES):
        yc = np.asarray(res.results[c]["y"])
        yp[2 * c] = yc[0]
        yp[2 * c + 1] = yc[1]
        ys[c] = yc[2]
    return (yp, ys)
```

```python
import contextlib
import numpy as np
import concourse.bass as bass
import concourse.mybir as mybir
from concourse.bass_utils import run_bass_kernel_spmd

F32 = mybir.dt.float32
BF16 = mybir.dt.bfloat16
ALU = mybir.AluOpType
AF = mybir.ActivationFunctionType

T = 2048
NCORES = 8
NSLOT = 3
RMS_EPS = 1e-6
LN_EPS = 1e-5
NOILV = False
DBG = 99
SKIPMIX = False


def _rel_bucket(rel):
    half = 16
    max_exact = 8
    n = np.abs(rel)
    large = max_exact + (np.log(np.maximum(n, 1) / max_exact) / np.log(1024 / max_exact) * (half - max_exact)).astype(np.int32)
    large = np.minimum(large, half - 1)
    return (rel > 0).astype(np.int32) * half + np.where(n < max_exact, n, large)


def _onehot_tables():
    out = np.zeros((33, 4, 512), np.float32)
    for t, (radius, dil) in enumerate(((128, 1), (64, 1), (64, 4), (64, 16))):
        rel = 255 - np.arange(512)
        b = _rel_bucket(rel * dil)
        valid = np.abs(rel) <= radius
        for u in range(512):
            out[b[u] if valid[u] else 32, t, u] = 1.0
    return out


class S:
    ENG = ('pe', 'act', 'dve', 'pool', 'sp')

    def __init__(self, nc, stack):
        self.nc = nc
        self.stack = stack
        self.ops = {e: [] for e in self.ENG}
        self.sems = {}
        self.cnt = {}
        self.res = {}
        self.waited = {e: {} for e in self.ENG}
        self.floor = {e: {} for e in self.ENG}
        for e in ('pe', 'act', 'dve', 'pool'):
            self.mksem(e)
        self.pe_open = False
        self.bar = {}
        self.psum = [nc.alloc_psum_tensor("ps%d" % i, [128, 512], F32) for i in range(8)]
        self.psi = 0
        self.nsem = 0

    def mksem(self, key):
        self.sems[key] = self.stack.enter_context(self.nc.semaphore("sm%d" % len(self.sems)))
        self.cnt[key] = 0

    def ps(self):
        i = self.psi % 8
        self.psi += 1
        return self.psum[i], ('ps', i)

    def _waits(self, eng, reads, writes):
        need = dict(self.floor[eng])

        def add(ev):
            if ev is None:
                return
            k, v = ev
            if need.get(k, 0) < v:
                need[k] = v
        for k in reads:
            r = self.res.get(k)
            if r is not None:
                add(r[0])
        for k in writes:
            r = self.res.get(k)
            if r is not None:
                add(r[0])
                for kk, vv in r[1].items():
                    add((kk, vv))
        out = []
        wd = self.waited[eng]
        for k, v in need.items():
            if eng == 'pe' and k == 'pe':
                continue
            if wd.get(k, 0) >= v:
                continue
            wd[k] = v
            out.append((k, v))
        self.floor[eng] = {}
        return out

    def _commit(self, ev, reads, writes):
        k, v = ev
        for key in reads:
            r = self.res.setdefault(key, [None, {}])
            if r[1].get(k, 0) < v:
                r[1][k] = v
        for key in writes:
            self.res[key] = [ev, {}]

    def op(self, eng, fn, reads=(), writes=(), signal=True):
        waits = self._waits(eng, reads, writes)
        if signal:
            self.cnt[eng] += 1
            ev = (eng, self.cnt[eng])
            if eng == 'pe':
                self.pe_open = False
        else:
            ev = (eng, self.cnt[eng] + 1)
            self.pe_open = True
        self.ops[eng].append((fn, waits, eng if signal else None, 1))
        self._commit(ev, reads, writes)

    def dma(self, q, semkey, fns, reads=(), writes=(), union=False):
        if semkey not in self.sems:
            self.mksem(semkey)
        if union:
            f = self.floor[q]
            for k, v in self.bar.items():
                if f.get(k, 0) < v:
                    f[k] = v
        waits = self._waits(q, reads, writes)
        for i, fn in enumerate(fns):
            self.cnt[semkey] += 16
            self.ops[q].append((fn, waits if i == 0 else [], semkey, 16))
        self._commit((semkey, self.cnt[semkey]), reads, writes)

    def barrier(self, dma_keys=()):
        assert not self.pe_open
        cur = {e: self.cnt[e] for e in ('pe', 'act', 'dve', 'pool')}
        for k in dma_keys:
            if k in self.cnt:
                cur[k] = self.cnt[k]
        for e in ('pe', 'act', 'dve', 'pool'):
            f = self.floor[e]
            for k, v in cur.items():
                if k != e and f.get(k, 0) < v:
                    f[k] = v
        for k, v in cur.items():
            if self.bar.get(k, 0) < v:
                self.bar[k] = v

    def final_wait(self, eng, keys):
        waits = [(k, self.cnt[k]) for k in keys if k in self.cnt]
        self.ops[eng].append((None, waits, None, 0))

    def emit(self, block):
        def run(name):
            def f(eng):
                for fn, waits, sk, inc in self.ops[name]:
                    for k, v in waits:
                        eng.wait_ge(self.sems[k], v)
                    if fn is None:
                        continue
                    ins = fn(eng)
                    if sk is not None:
                        ins.then_inc(self.sems[sk], inc)
            return f
        block.tensor(run('pe'))
        block.scalar(run('act'))
        block.vector(run('dve'))
        block.gpsimd(run('pool'))
        block.sync(run('sp'))

    def mmg(self, out_ap, key, pairs, first=True, last=True):
        n = len(pairs)
        for i, (l, r, rd) in enumerate(pairs):
            st = first and i == 0
            sp = last and i == n - 1
            self.op('pe', (lambda e, l=l, r=r, st=st, sp=sp: e.matmul(out_ap, lhsT=l, rhs=r, start=st, stop=sp)),
                    reads=rd, writes=[key], signal=(i == n - 1))


def build(nseq=3, layers=(0, 1, 2, 3)):
    nc = bass.Bass("TRN2", target_bir_lowering=False)
    stack = contextlib.ExitStack()

    def din(name, shape, dtype=F32):
        return nc.dram_tensor(name, list(shape), dtype, kind="ExternalInput").ap()

    x = din("x", [nseq, T, 1024])
    y = nc.dram_tensor("y", [nseq, T, 1024], F32, kind="ExternalOutput").ap()
    rel_bias = din("rel_bias", [32, 16])
    prm_in = din("prm", [13, 1024])
    ffn_w1 = din("ffn_w1", [4, 1024, 4096])
    ffn_w2 = din("ffn_w2", [4, 4096, 1024])
    a_wqkv = din("a_wqkv", [2, 1024, 1536])
    a_sink = din("a_sink", [1, 32])
    a_wo = din("a_wo", [2, 1024, 1024])
    b_win = din("b_win", [1, 1024, 4096])
    b_ws = din("b_ws", [8, 128, 128])
    b_bs = din("b_bs", [1, 1024])
    b_wo = din("b_wo", [1, 2048, 1024])
    c_wqkv = din("c_wqkv", [1, 1024, 4608])
    c_wo = din("c_wo", [1, 1024, 1024])
    ident_in = din("ident", [128, 128])
    oh_in = din("oh", [33, 4 * 512])

    def dscr(name, shape, dtype=BF16):
        return nc.dram_tensor(name, list(shape), dtype, kind="Internal").ap()

    w1b = dscr("w1b", [4, 1024, 4096])
    w2b = dscr("w2b", [4, 4096, 1024])
    aqkvb = dscr("aqkvb", [2, 1024, 1536])
    awob = dscr("awob", [2, 1024, 1024])
    bwinb = dscr("bwinb", [1, 1024, 4096])
    bwob = dscr("bwob", [1, 2048, 1024])
    cqkvb = dscr("cqkvb", [1, 1024, 4608])
    cwob = dscr("cwob", [1, 1024, 1024])
    tabs_t = nc.dram_tensor("tabs", [4, 16, 128, 512], BF16, kind="Internal")
    tabs = tabs_t.ap()

    xT = nc.alloc_sbuf_tensor("xT", [128, 8, T], F32)
    hT = nc.alloc_sbuf_tensor("hT", [128, 8, T], BF16)
    colprm = nc.alloc_sbuf_tensor("colprm", [128, 8, 16], F32)
    ident = nc.alloc_sbuf_tensor("identf", [128, 128], F32)
    ones = nc.alloc_sbuf_tensor("onesb", [128, 128], BF16)
    es = nc.alloc_sbuf_tensor("es", [128, 32], F32)
    WsT = nc.alloc_sbuf_tensor("WsT", [128, 8, 128], BF16)
    Cterm = nc.alloc_sbuf_tensor("Cterm", [128, 16, 128], F32)
    UBYTES = 102400
    U = nc.alloc_sbuf_tensor("U", [128, UBYTES // 2], BF16)

    def uv(off, shape, dtype):
        n = 1
        for d in shape:
            n *= d
        nb = n * (4 if dtype == F32 else 2)
        assert off % 4 == 0 and off + nb <= UBYTES, (off, nb)
        a = U[:, off // 2:(off + nb) // 2]
        if dtype == F32:
            a = a.bitcast(F32)
        if len(shape) == 2:
            return a.rearrange("p (a b) -> p a b", a=shape[0])
        if len(shape) == 3:
            return a.rearrange("p (a b c) -> p a b c", a=shape[0], b=shape[1])
        return a

    wslot = [uv(i * 8192, [8, 512], BF16) for i in range(NSLOT)]
    R = NSLOT * 8192

    with stack:
        s = S(nc, stack)
        block = stack.enter_context(nc.Block())
        s.wsi = 0

        cast_q = []

        def cast(unit, dst2d, src2d, rows, cols):
            step = max(1, (256 * 1024) // cols)
            chunks = [(r0, min(rows, r0 + step)) for r0 in range(0, rows, step)]
            for ci, (r0, r1) in enumerate(chunks):
                cast_q.append((unit, (lambda e, r0=r0, r1=r1: e.dma_start(out=dst2d[r0:r1, :], in_=src2d[r0:r1, :])), ci == len(chunks) - 1))

        def cast_tick(n=1):
            for _ in range(n):
                if not cast_q:
                    return
                unit, fn, last = cast_q.pop(0)
                key = ('cast', unit)
                s.dma('pool', key, [fn])
                if last:
                    s.res[('wb', unit)] = [(key, s.cnt[key]), {}]

        def cast_need(unit):
            while ('wb', unit) not in s.res:
                assert cast_q
                cast_tick(1)

        def cast_layer(i):
            kind, j = i % 3, i // 3
            if kind == 0:
                cast(('aqkv', j), aqkvb[j], a_wqkv[j], 1024, 1536)
                cast(('awo', j), awob[j], a_wo[j], 1024, 1024)
            elif kind == 1:
                cast(('bwin', 0), bwinb[0], b_win[0], 1024, 4096)
                cast(('bwo', 0), bwob[0], b_wo[0], 2048, 1024)
            else:
                cast(('cqkv', 0), cqkvb[0], c_wqkv[0], 1024, 4608)
                cast(('cwo', 0), cwob[0], c_wo[0], 1024, 1024)
            cast(('w1', i), w1b[i], ffn_w1[i], 1024, 4096)
            cast(('w2', i), w2b[i], ffn_w2[i], 4096, 1024)

        prm_rows = uv(R, [1024], F32)
        rb33 = uv(R + 4096, [16], F32)
        rbrep = uv(R + 4096 + 1024, [128], F32)
        ohs = uv(R + 8192, [4, 512], F32)
        etab = uv(UBYTES - 4096, [4, 512], BF16)
        wsf = uv(R + 20480, [8, 128], F32)
        bsb = uv(R + 24576, [8, 128], F32)
        sinkb = uv(R + 28672, [32], F32)
        cfn = [
            lambda e: e.dma_start(out=ident[:], in_=ident_in),
            lambda e: e.dma_start(out=prm_rows[0:13, :], in_=prm_in),
            lambda e: e.dma_start(out=rb33[0:32, :], in_=rel_bias),
            lambda e: e.dma_start(out=ohs[0:33, :, :], in_=oh_in.rearrange("p (a b) -> p a b", a=4)),
            lambda e: e.dma_start(out=wsf[:, :, :], in_=b_ws.rearrange("g p q -> p g q")),
            lambda e: e.dma_start(out=bsb[:, :, :], in_=b_bs[0, :].partition_broadcast(128).rearrange("p (g q) -> p g q", g=8)),
            lambda e: e.dma_start(out=sinkb[:, :], in_=a_sink[0, :].partition_broadcast(128)),
        ]
        s.dma('sp', 'c0', cfn, reads=[], writes=['c_in'])
        for i in layers:
            cast_layer(i)
        if layers:
            cast_tick(12)
        if True:
            s.op('dve', lambda e: e.memset(ones[:], 1.0), writes=['ones'])
        if DBG >= 2:
            s.op('dve', lambda e: e.memset(rb33[32:33, :], -30000.0), reads=[], writes=['rb_m'])
            s.op('act', lambda e: e.activation(out=es[:], in_=sinkb[:, :], func=AF.Exp), reads=['c_in'], writes=['es'])
            pst, pk = s.ps()
            for c in range(8):
                s.op('pe', lambda e, c=c: e.transpose(out=pst[:, c * 16:c * 16 + 13], in_=prm_rows[0:13, c * 128:(c + 1) * 128], identity=ident[0:13, 0:13]),
                     reads=['c_in'], writes=[pk], signal=(c == 7))
            s.op('dve', lambda e: e.tensor_copy(out=colprm[:, :, 0:13], in_=pst[:, 0:128].rearrange("p (c k) -> p c k", c=8)[:, :, 0:13]),
                 reads=[pk], writes=['colprm'])
        if DBG >= 3:
            s.op('dve', lambda e: e.tensor_copy(out=rbrep[0:33, :].rearrange("p (h r) -> p h r", r=8), in_=rb33[0:33, :].unsqueeze(2).broadcast_to([33, 16, 8])),
                 reads=['c_in', 'rb_m'], writes=['rbrep'])
            for t in range(4):
                pt_, ptk = s.ps()
                s.op('pe', lambda e, t=t, pt_=pt_: e.matmul(pt_[:, :], lhsT=rbrep[0:33, :], rhs=ohs[0:33, t, :], start=True, stop=True),
                     reads=['c_in', 'rbrep'], writes=[ptk])
                s.op('act', lambda e, t=t, pt_=pt_: e.activation(out=etab[:, t, :], in_=pt_[:, :], func=AF.Exp), reads=[ptk], writes=[('etab', t)])
        if DBG >= 4:
            for t in range(4):
                tfn = [lambda e, t=t: e.dma_start(out=tabs[t].rearrange("h (r j) u -> (h r) j u", r=8), in_=etab[:, t, :].unsqueeze(1).broadcast_to([128, 16, 512]))]
                s.dma('sp', ('c1', t), tfn, reads=[('etab', t)], writes=[('tabs', t)])
        if DBG >= 5:
            for half in range(2):
                pw, pwk = s.ps()
                for j in range(4):
                    g = half * 4 + j
                    s.op('pe', lambda e, g=g, j=j, pw=pw: e.transpose(out=pw[:, j * 128:(j + 1) * 128], in_=wsf[:, g, :], identity=ident[:]),
                         reads=['c_in'], writes=[pwk], signal=(j == 3))
                s.op('act', lambda e, half=half, pw=pw: e.activation(out=WsT[:, half * 4:half * 4 + 4, :], in_=pw[:, :].rearrange("p (a b) -> p a b", a=4), func=AF.Copy),
                     reads=[pwk], writes=[('WsT', half)])
            for half in range(2):
                pr, prk = s.ps()
                for j in range(4):
                    g = half * 4 + j
                    s.op('pe', lambda e, g=g, j=j, pr=pr: e.matmul(pr[:, j * 128:(j + 1) * 128], lhsT=ones[:], rhs=WsT[:, g, :], start=True, stop=True),
                         reads=['ones', ('WsT', half)], writes=[prk], signal=(j == 3))
                for j in range(4):
                    g = half * 4 + j
                    for k in range(2):
                        fc = g * 2 + k
                        s.op('dve', lambda e, g=g, j=j, fc=fc, pr=pr: e.scalar_tensor_tensor(
                            out=Cterm[:, fc, :], in0=pr[:, j * 128:(j + 1) * 128], scalar=colprm[:, fc % 8, 11 + fc // 8:12 + fc // 8],
                            in1=bsb[:, g, :], op0=ALU.mult, op1=ALU.add), reads=[prk, 'colprm', 'c_in'], writes=[('Cterm', fc)])
        s.barrier(['c0'])

        def wload(parts, unit):
            cast_need(unit)
            slot = s.wsi % NSLOT
            s.wsi += 1
            sl = wslot[slot]
            fns = [(lambda e, o=ov(sl), d=dv: e.dma_start(out=o, in_=d)) for ov, dv in parts]
            s.dma('sp', ('wsem', slot), fns, reads=[('wb', unit)], writes=[('ws', slot)])
            return sl, ('ws', slot)

        def std_tile(w2d, r0, c0, ncols=512, nk=8):
            return [(lambda sl: sl[:, 0:nk, 0:ncols], w2d[r0:r0 + nk * 128, c0:c0 + ncols].rearrange("(k p) n -> p k n", p=128))]

        def stream(specs, compute, look=2):
            loaded = []
            n = len(specs)
            nxt = 0
            for i in range(n):
                while nxt < n and nxt <= i + look - 1:
                    loaded.append(wload(*specs[nxt]))
                    nxt += 1
                compute(i, *loaded[i])

        def rmsnorm_stats(rstd):
            sq = [uv(R + 8192, [T], BF16), uv(R + 12288, [T], BF16)]
            pss = [s.ps() for _ in range(4)]
            for c in range(8):
                b = sq[c % 2]
                s.op('act', lambda e, c=c, b=b: e.activation(out=b[:, :], in_=xT[:, c, :], func=AF.Square),
                     reads=[('xT', c, tb) for tb in range(4)], writes=[('sq', c % 2)])
                for tb in range(4):
                    p_, k_ = pss[tb]
                    s.mmg(p_[:, :], k_, [(ones[:], b[:, tb * 512:(tb + 1) * 512], [('sq', c % 2), 'ones'])], first=(c == 0), last=(c == 7))
            for tb in range(4):
                p_, k_ = pss[tb]
                s.op('act', lambda e, tb=tb, p_=p_: e.activation(out=rstd[:, tb * 512:(tb + 1) * 512], in_=p_[:, :], func=AF.Ln, bias=RMS_EPS, scale=1.0 / 1024),
                     reads=[k_], writes=[('rstd', tb)])
                s.op('act', lambda e, tb=tb: e.activation(out=rstd[:, tb * 512:(tb + 1) * 512], in_=rstd[:, tb * 512:(tb + 1) * 512], func=AF.Exp, scale=-0.5),
                     reads=[('rstd', tb)], writes=[('rstd', tb)])

        def rmsnorm(gi):
            rstd = uv(R, [T], F32)
            rmsnorm_stats(rstd)
            for c in range(8):
                eng = 'dve'
                s.op(eng, lambda e, c=c: e.scalar_tensor_tensor(out=hT[:, c, :], in0=xT[:, c, :], scalar=colprm[:, c, gi:gi + 1], in1=rstd[:, :],
                                                                 op0=ALU.mult, op1=ALU.mult),
                     reads=[('xT', c, tb) for tb in range(4)] + [('rstd', tb) for tb in range(4)] + ['colprm'], writes=[('hT', c)])
            s.barrier()

        def norm_sq(tb, sqbase):
            tsl = slice(tb * 512, (tb + 1) * 512)
            for c in range(8):
                bq = uv(sqbase + c * 1024, [512], BF16)
                s.op('act', lambda e, c=c, bq=bq: e.activation(out=bq[:, :], in_=xT[:, c, tsl], func=AF.Square), reads=[('xT', c, tb)], writes=[('nsq8', c)])

        def norm_tb(gi, tb, scr, final=False, sq8=None):
            rstd = uv(scr, [512], F32)
            sq = [uv(scr + 2048, [512], BF16), uv(scr + 3072, [512], BF16)]
            tsl = slice(tb * 512, (tb + 1) * 512)
            p_, k_ = s.ps()
            for c in range(8):
                if sq8 is not None:
                    bq = uv(sq8 + c * 1024, [512], BF16)
                    s.mmg(p_[:, :], k_, [(ones[:], bq[:, :], [('nsq8', c), 'ones'])], first=(c == 0), last=(c == 7))
                    continue
                bq = sq[c % 2]
                s.op('act', lambda e, c=c, bq=bq: e.activation(out=bq[:, :], in_=xT[:, c, tsl], func=AF.Square), reads=[('xT', c, tb)], writes=[('nsq', c % 2)])
                s.mmg(p_[:, :], k_, [(ones[:], bq[:, :], [('nsq', c % 2), 'ones'])], first=(c == 0), last=(c == 7))
            s.op('act', lambda e: e.activation(out=rstd[:, :], in_=p_[:, :], func=AF.Ln, bias=RMS_EPS, scale=1.0 / 1024), reads=[k_], writes=['nrstd'])
            s.op('act', lambda e: e.activation(out=rstd[:, :], in_=rstd[:, :], func=AF.Exp, scale=-0.5), reads=['nrstd'], writes=['nrstd'])
            for c in range(8):
                dst = xT if final else hT
                wk = ('xT', c, tb) if final else ('hT', c, tb)
                s.op('dve', lambda e, c=c, dst=dst: e.scalar_tensor_tensor(out=dst[:, c, tsl], in0=xT[:, c, tsl], scalar=colprm[:, c, gi:gi + 1], in1=rstd[:, :],
                                                                          op0=ALU.mult, op1=ALU.mult),
                     reads=[('xT', c, tb), 'nrstd', 'colprm'], writes=[wk])

        def store_tb(sq_, tb, ybase):
            yb = [uv(ybase + j * 4096, [1024], F32) for j in range(2)]
            for q4 in range(4):
                tt = tb * 4 + q4
                j = tt % 2
                for half in range(2):
                    p_, k_ = s.ps()
                    for q in range(4):
                        c = half * 4 + q
                        s.op('pe', lambda e, c=c, q=q, tt=tt, p_=p_: e.transpose(out=p_[:, q * 128:(q + 1) * 128], in_=xT[:, c, tt * 128:(tt + 1) * 128], identity=ident[:]),
                             reads=[('xT', c, tb)], writes=[k_], signal=(q == 3))
                    if half == 0:
                        s.op('act', lambda e, j=j, p_=p_: e.activation(out=yb[j][:, 0:512], in_=p_[:, :], func=AF.Copy), reads=[k_], writes=[('yout', j, 0)])
                    else:
                        s.op('dve', lambda e, j=j, p_=p_: e.tensor_copy(out=yb[j][:, 512:1024], in_=p_[:, :]), reads=[k_], writes=[('yout', j, 1)])
                s.dma('sp', ('yst', j), [lambda e, tt=tt, j=j: e.dma_start(out=y[sq_, tt * 128:(tt + 1) * 128, :], in_=yb[j][:, :])],
                      reads=[('yout', j, 0), ('yout', j, 1)], writes=[], union=True)

        def load_dma_tb(sq_, tb, xbase):
            xb = [uv(xbase + q * 4096, [1024], F32) for q in range(4)]
            for q4 in range(4):
                tt = tb * 4 + q4
                s.dma('sp', ('xin2', q4), [lambda e, tt=tt, q4=q4: e.dma_start(out=xb[q4][:, :], in_=x[sq_, tt * 128:(tt + 1) * 128, :])],
                      reads=[], writes=[('xin2', q4)], union=True)

        def load_tr_tb(tb, xbase):
            xb = [uv(xbase + q * 4096, [1024], F32) for q in range(4)]
            for q4 in range(4):
                tt = tb * 4 + q4
                for half in range(2):
                    p_, k_ = s.ps()
                    for q in range(4):
                        c = half * 4 + q
                        s.op('pe', lambda e, q4=q4, c=c, q=q, p_=p_: e.transpose(out=p_[:, q * 128:(q + 1) * 128], in_=xb[q4][:, c * 128:(c + 1) * 128], identity=ident[:]),
                             reads=[('xin2', q4)], writes=[k_], signal=(q == 3))
                    wk = [('xT', half * 4 + q, tb) for q in range(4)]
                    if half == 0:
                        s.op('act', lambda e, half=half, tt=tt, p_=p_: e.activation(out=xT[:, half * 4:half * 4 + 4, tt * 128:(tt + 1) * 128],
                                                                              in_=p_[:, :].rearrange("p (a b) -> p a b", a=4), func=AF.Copy), reads=[k_], writes=wk)
                    else:
                        s.op('dve', lambda e, half=half, tt=tt, p_=p_: e.tensor_copy(out=xT[:, half * 4:half * 4 + 4, tt * 128:(tt + 1) * 128],
                                                                               in_=p_[:, :].rearrange("p (a b) -> p a b", a=4)), reads=[k_], writes=wk)

        class Nxt:
            def __init__(self, kind, arg, nload=None):
                self.kind, self.arg, self.nload = kind, arg, nload
                self.sq = {}

            def pre(self, tb, xbase, sqbase=None):
                if sqbase is not None:
                    norm_sq(tb, sqbase)
                    self.sq[tb] = sqbase
                if self.kind == 'final' and self.nload is not None:
                    load_dma_tb(self.nload[0], tb, xbase)

            def emit(self, tb, scr, ybase, xbase=None):
                if self.kind == 'norm':
                    norm_tb(self.arg, tb, scr, sq8=self.sq.get(tb))
                else:
                    norm_tb(8, tb, scr, final=True, sq8=self.sq.get(tb))
                    store_tb(self.arg, tb, ybase)
                    if self.nload is not None and xbase is not None:
                        load_tr_tb(tb, xbase)
                        norm_tb(self.nload[1], tb, scr)

        def resid_add(m, tb, p_, k_):
            s.op('dve', lambda e: e.tensor_tensor(out=xT[:, m, tb * 512:(tb + 1) * 512], in0=xT[:, m, tb * 512:(tb + 1) * 512], in1=p_[:, :], op=ALU.add),
                 reads=[k_, ('xT', m, tb)], writes=[('xT', m, tb)])

        def ffn(i, nxt):
            scr, ybase, xbase = R + 40960, R + 45056, R + 53248
            aT = uv(R, [32, 512], BF16)
            rtmp = [uv(R + 32768 + j * 2048, [512], F32) for j in range(4)]
            rcnt = [0]
            for tb in range(4):
                tsl = slice(tb * 512, (tb + 1) * 512)

                def c1(cb, sl, sk):
                    for m in range(4):
                        p_, k_ = s.ps()
                        s.mmg(p_[:, :], k_, [(sl[:, kc, m * 128:(m + 1) * 128], hT[:, kc, tsl], [sk, ('hT', kc, tb)]) for kc in range(8)])
                        j = rcnt[0] % 4
                        rcnt[0] += 1
                        rt = rtmp[j]
                        fc = cb * 4 + m
                        s.op('act', lambda e, rt=rt, p_=p_: e.activation(out=rt[:, :], in_=p_[:, :], func=AF.Relu), reads=[k_], writes=[('rt', j)])
                        s.op('pool', lambda e, rt=rt, fc=fc: e.tensor_tensor(out=aT[:, fc, :], in0=rt[:, :], in1=rt[:, :], op=ALU.mult),
                             reads=[('rt', j)], writes=[('aT', fc)])
                if tb >= 1:
                    nxt.pre(tb - 1, xbase, R + 69632)
                stream([(std_tile(w1b[i], 0, cb * 512), ('w1', i)) for cb in range(8)], c1)
                if tb >= 1:
                    nxt.emit(tb - 1, scr, ybase, xbase)
                for cb in range(2):
                    pss = [s.ps() for _ in range(4)]

                    def c2(kg, sl, sk, cb=cb, pss=pss):
                        for m in range(4):
                            p_, k_ = pss[m]
                            s.mmg(p_[:, :], k_, [(sl[:, kc, m * 128:(m + 1) * 128], aT[:, kg * 8 + kc, :], [sk, ('aT', kg * 8 + kc)]) for kc in range(8)],
                                  first=(kg == 0), last=(kg == 3))
                            if kg == 3:
                                resid_add(cb * 4 + m, tb, p_, k_)
                    stream([(std_tile(w2b[i], kg * 1024, cb * 512), ('w2', i)) for kg in range(4)], c2)
            nxt.pre(3, xbase, R + 69632)
            nxt.emit(3, scr, ybase, xbase)
            s.barrier([('yst', 0), ('yst', 1)] + [('xin2', q) for q in range(4)])

        def attn_bufs(base):
            b = {}
            b['qT'] = uv(base, [2, T], BF16)
            b['kT'] = uv(base + 8192, [T], BF16)
            b['V'] = uv(base + 12288, [16, 192], BF16)
            b['EB'] = uv(base + 18432, [3, 512], BF16)
            b['pT'] = [uv(base + 21504 + j * 1024, [512], BF16) for j in range(6)]
            b['e'] = [uv(base + 27648 + j * 1024, [512], BF16) for j in range(3)]
            b['oT'] = uv(base + 30720, [2, T], BF16)
            b['tmp'] = uv(base + 38912, [256], F32)
            b['rec'] = uv(base + 39936, [256], F32)
            b['kTo'] = uv(base + 40960, [T], BF16)
            b['end'] = base + 45056
            return b

        def project_qkv(b, wb2d, unit, qc0, kc0, vc0, perm):
            parts = [(lambda sl: sl[:, :, 0:256], wb2d[:, qc0:qc0 + 256].rearrange("(k p) n -> p k n", p=128)),
                     (lambda sl: sl[:, :, 256:320], wb2d[:, kc0:kc0 + 64].rearrange("(k p) n -> p k n", p=128)),
                     (lambda sl: sl[:, :, 320:384], wb2d[:, kc0:kc0 + 64].rearrange("(k p) n -> p k n", p=128)),
                     (lambda sl: sl[:, :, 384:448], wb2d[:, vc0:vc0 + 64].rearrange("(k p) n -> p k n", p=128))]
            sl, sk = wload(parts, unit)
            tg = b.get('tag', '')
            D = perm
            L = T // D
            cnt = 0
            for mi in range(3):
                for tb in range(4):
                    p_, k_ = s.ps()
                    s.mmg(p_[:, :], k_, [(sl[:, kc, mi * 128:(mi + 1) * 128], hT[:, kc, tb * 512:(tb + 1) * 512], [sk, ('hT', kc, tb)]) for kc in range(8)])
                    if mi < 2:
                        jobs = [(b['qT'][:, mi, :], p_[:, :], (tg + 'qT', mi, tb), 0.125)]
                    else:
                        jobs = [(b['kT'][0:64, :], p_[0:64, :], (tg + 'kT', tb), 1.0), (b['kTo'][64:128, :], p_[64:128, :], (tg + 'kTo', tb), 1.0)]
                    for dst, src, wkey, sc in jobs:
                        if D == 1:
                            o_ap = dst[:, tb * 512:(tb + 1) * 512]
                            i_ap = src
                        else:
                            n_i = 512 // D
                            o_ap = dst.rearrange("p (r i) -> p r i", r=D)[:, :, tb * n_i:(tb + 1) * n_i]
                            i_ap = src.rearrange("p (i r) -> p r i", r=D)
                        if cnt % 2 == 0:
                            s.op('act', lambda e, o_ap=o_ap, i_ap=i_ap, sc=sc: e.activation(out=o_ap, in_=i_ap, func=AF.Copy, scale=sc), reads=[k_], writes=[wkey])
                        else:
                            s.op('dve', lambda e, o_ap=o_ap, i_ap=i_ap, sc=sc: e.tensor_scalar(out=o_ap, in0=i_ap, scalar1=sc, scalar2=None, op0=ALU.mult), reads=[k_], writes=[wkey])
                        cnt += 1
                    yield
            for half in range(2):
                p_, k_ = s.ps()
                for j in range(8):
                    tt = half * 8 + j
                    r, idx0 = (tt * 128) // L, (tt * 128) % L
                    t0 = idx0 * D + r
                    s.mmg(p_[:, j * 64:(j + 1) * 64], k_,
                          [(hT[:, kc, t0:t0 + 127 * D + 1:D], sl[:, kc, 384:448], [sk] + ([('hT', kc, t0 // 512)] if D == 1 else [('hT', kc, q_) for q_ in range(4)])) for kc in range(8)])
                s.op('act', lambda e, half=half, p_=p_: e.activation(out=b['V'][:, half * 8:half * 8 + 8, 64:128], in_=p_[:, :].rearrange("p (a b) -> p a b", a=8), func=AF.Copy),
                     reads=[k_], writes=[(tg + 'V', half)])
                yield

        def attn_core(b, nbs, pv_evac):
            qT, kT, V, EB = b['qT'], b['kT'], b['V'], b['EB']
            tg = b.get('tag', '')
            pcnt = [0]
            ecnt = [0]

            def s_stage(qb):
                outp = []
                for bt in range(3):
                    kb = qb + bt - 1
                    if kb // nbs != qb // nbs or kb < 0 or kb >= 16:
                        continue
                    p_, k_ = s.ps()
                    tq = qb // 4
                    tk = kb // 4
                    qs = slice(qb * 128, (qb + 1) * 128)
                    ks = slice(kb * 128, (kb + 1) * 128)
                    s.mmg(p_[:, 0:256].rearrange("p (a b) -> p a b", a=2), k_, [(kT[:, ks], qT[:, :, qs], [(tg + 'kT', tk), tg + 'kzero', (tg + 'qT', 0, tq), (tg + 'qT', 1, tq)])])
                    s.mmg(p_[:, 256:512].rearrange("p (a b) -> p a b", a=2), k_, [(b['kTo'][:, ks], qT[:, :, qs], [(tg + 'kTo', tk), tg + 'kzero', (tg + 'qT', 0, tq), (tg + 'qT', 1, tq)])])
                    je = ecnt[0] % 3
                    ecnt[0] += 1
                    jp = pcnt[0] % 6
                    pcnt[0] += 1
                    eb_, pT_ = b['e'][je], b['pT'][jp]
                    s.op('act', lambda e, eb_=eb_, p_=p_: e.activation(out=eb_[:, :], in_=p_[:, :], func=AF.Exp), reads=[k_], writes=[('e', je)])
                    eng = 'dve' if (pcnt[0] % 2 == 0) else 'pool'
                    if DBG >= 13.2:
                        s.op(eng, lambda e, eb_=eb_, pT_=pT_, bt=bt: e.tensor_tensor(out=pT_[:, :], in0=eb_[:, :], in1=EB[:, bt, :], op=ALU.mult),
                             reads=[('e', je), 'EB'], writes=[('pT', jp)])
                    outp.append((kb, pT_, ('pT', jp)))
                return outp

            def pv_stage(qb, plist):
                if DBG < 13.4:
                    return
                pa, ka = s.ps()
                pb, kb_ = s.ps()
                n = len(plist)
                for i, (kb, pT_, pk) in enumerate(plist):
                    s.mmg(pa[:, 0:256], ka, [(V[:, kb, 64:192], pT_[:, 0:256], [pk, (tg + 'V', kb // 8), tg + 'Vones'])], first=(i == 0), last=(i == n - 1))
                    s.mmg(pb[:, 0:256], kb_, [(V[:, kb, 0:128], pT_[:, 256:512], [pk, (tg + 'V', kb // 8), tg + 'Vones'])], first=(i == 0), last=(i == n - 1))
                if DBG >= 13.6:
                    pv_evac(qb, pa, ka, pb, kb_)

            prev = s_stage(0)
            for qb in range(16):
                nxt = s_stage(qb + 1) if qb + 1 < 16 else None
                pv_stage(qb, prev)
                prev = nxt
                cast_tick(2)
                yield

        def wo_partial(b, wob2d, unit, g, after_tb=None, before_tb=None):
            oT = b['oT']
            tg = b.get('tag', '')
            cast_need(unit)
            slot = s.wsi % NSLOT
            s.wsi += 1
            sl = wslot[slot]
            slv = sl[:, 0:4, :].rearrange("p a b -> p (a b)").rearrange("p (k n) -> p k n", k=2)
            s.dma('sp', ('wsem', slot), [lambda e: e.dma_start(out=slv, in_=wob2d[g * 256:(g + 1) * 256, :].rearrange("(k p) n -> p k n", p=128))],
                  reads=[('wb', unit)], writes=[('ws', slot)])
            sk = ('ws', slot)
            for tb in range(4):
                if before_tb is not None and tb >= 1:
                    before_tb(tb - 1)
                for m in range(8):
                    p_, k_ = s.ps()
                    s.mmg(p_[:, :], k_, [(slv[:, kc, m * 128:(m + 1) * 128], oT[:, kc, tb * 512:(tb + 1) * 512], [sk, (tg + 'oT', kc, tb)]) for kc in range(2)])
                    resid_add(m, tb, p_, k_)
                    yield
                if after_tb is not None and tb >= 1:
                    after_tb(tb - 1)
            if before_tb is not None:
                before_tb(3)
            if after_tb is not None:
                after_tb(3)

        def load_EB(b, t, g):
            fns = []
            for bt in range(3):
                o_bt = 255 - (bt - 1) * 128
                for par in range(2):
                    src = bass.AP(tabs_t, t * 16 * 128 * 512 + (4 * g + par) * 128 * 512 + o_bt, [[511, 128], [2 * 128 * 512, 2], [1, 128]])
                    fns.append(lambda e, bt=bt, par=par, src=src: e.dma_start(
                        out=b['EB'][:, bt, par * 256:(par + 1) * 256].rearrange("p (c i) -> p c i", c=2), in_=src))
            s.dma('sp', 'ebsem', fns, reads=[('tabs', t)], writes=['EB'], union=True)

        def v_ones(b):
            tg = b.get('tag', '')
            s.op('pool', lambda e: e.memset(b['V'][:, :, 0:64], 1.0), writes=[tg + 'Vones'])
            s.op('pool', lambda e: e.memset(b['V'][:, :, 128:192], 1.0), writes=[tg + 'Vones'])
            s.op('pool', lambda e: e.memset(b['kT'][64:128, :], 0.0), writes=[tg + 'kzero'])
            s.op('pool', lambda e: e.memset(b['kTo'][0:64, :], 0.0), writes=[tg + 'kzero'])

        def mixer_a(i, j):
            s.barrier([('c1', t) for t in range(4)])
            SB = R + 45056
            shared = {'EB': uv(SB, [3, 512], BF16),
                      'pT': [uv(SB + 3072 + q * 1024, [512], BF16) for q in range(6)],
                      'e': [uv(SB + 9216 + q * 1024, [512], BF16) for q in range(3)]}
            lnd, rec = uv(SB + 12288, [256], F32), uv(SB + 13312, [256], F32)
            sets = []
            for st in range(2):
                base = R + st * 22528
                d = dict(shared)
                d.update({'tag': 'A%d' % st, 'qT': uv(base, [2, T], BF16), 'kT': uv(base + 8192, [T], BF16), 'kTo': uv(base + 12288, [T], BF16),
                          'V': uv(base + 16384, [16, 192], BF16), 'oT': uv(SB + 14336 + st * 8192, [2, T], BF16)})
                sets.append(d)
                v_ones(d)

            def proj(g):
                return project_qkv(sets[g % 2], aqkvb[j], ('aqkv', j), g * 256, 1024 + g * 64, 1280 + g * 64, 1)

            def mk_evac(g, b):
                oT, tg = b['oT'], b['tag']

                def pv_evac(qb, pa, ka, pb, kb_):
                    tq = qb // 4
                    qs = slice(qb * 128, (qb + 1) * 128)
                    okeys = [(tg + 'oT', 0, tq), (tg + 'oT', 1, tq)]
                    for mq in range(2):
                        ha = j * 16 + 4 * g + 2 * mq
                        s.op('act', lambda e, mq=mq, ha=ha: e.activation(out=lnd[64:128, mq * 128:(mq + 1) * 128], in_=pa[64:128, mq * 128:(mq + 1) * 128],
                                                                         func=AF.Ln, bias=es[64:128, ha:ha + 1]), reads=[ka, 'es'], writes=['lnda'])
                    s.op('act', lambda e: e.activation(out=rec[0:64, :], in_=lnd[64:128, :], func=AF.Exp, scale=-1.0), reads=['lnda'], writes=['reca'])
                    s.op('dve', lambda e: e.tensor_tensor(out=oT[0:64, :, qs], in0=pa[0:64, 0:256].rearrange("p (a b) -> p a b", a=2),
                                                          in1=rec[0:64, :].rearrange("p (a b) -> p a b", a=2), op=ALU.mult),
                         reads=[ka, 'reca'], writes=okeys)
                    for mq in range(2):
                        hb = j * 16 + 4 * g + 2 * mq + 1
                        s.op('act', lambda e, mq=mq, hb=hb: e.activation(out=lnd[0:64, mq * 128:(mq + 1) * 128], in_=pb[0:64, mq * 128:(mq + 1) * 128],
                                                                         func=AF.Ln, bias=es[0:64, hb:hb + 1]), reads=[kb_, 'es'], writes=['lndb'])
                    s.op('act', lambda e: e.activation(out=rec[64:128, :], in_=lnd[0:64, :], func=AF.Exp, scale=-1.0), reads=['lndb'], writes=['recb'])
                    s.op('dve', lambda e: e.tensor_tensor(out=oT[64:128, :, qs], in0=pb[64:128, 0:256].rearrange("p (a b) -> p a b", a=2),
                                                          in1=rec[64:128, :].rearrange("p (a b) -> p a b", a=2), op=ALU.mult),
                         reads=[kb_, 'recb'], writes=okeys)
                return pv_evac

            for _ in proj(0):
                pass
            pending = None
            for g in range(4):
                b = sets[g % 2]
                load_EB(b, 0, g)
                side = []
                if g + 1 < 4:
                    side.append((proj(g + 1), 1))
                if pending is not None:
                    side.append((pending, 2))
                if NOILV:
                    for sg, n in side:
                        for _q in sg:
                            pass
                    side = []
                for _ in attn_core(b, 16, mk_evac(g, b)):
                    for sg, n in list(side):
                        for _q in range(n):
                            try:
                                next(sg)
                            except StopIteration:
                                side = [x for x in side if x[0] is not sg]
                                break
                for sg, n in side:
                    for _q in sg:
                        pass
                if g < 3:
                    pending = wo_partial(b, awob[j], ('awo', j), g)
            s.barrier(['ebsem'])
            nf = Nxt('norm', 4 + i)
            for _ in wo_partial(sets[1], awob[j], ('awo', j), 3, after_tb=lambda tb_: nf.emit(tb_, R, 0), before_tb=lambda tb_: nf.pre(tb_, None, R + 8192)):
                pass
            s.barrier(['ebsem'])

        def mixer_c(i):
            s.barrier([('c1', t) for t in range(4)])
            b = attn_bufs(R)
            acc_a = uv(b['end'], [2, T], F32)
            acc_b = uv(b['end'] + 16384, [2, T], F32)
            reca = uv(R + 38912, [2, 256], F32)
            v_ones(b)
            oT = b['oT']
            def proj(g_, dg_):
                base = dg_ * 1536
                for _ in project_qkv(b, cqkvb[0], ('cqkv', 0), base + g_ * 256, base + 1024 + g_ * 64, base + 1280 + g_ * 64, (1, 4, 16)[dg_]):
                    pass

            proj(0, 0)
            for g in range(4):
                for dg, D in enumerate((1, 4, 16)):
                    L = T // D
                    load_EB(b, 1 + dg, g)

                    def pv_evac(qb, pa, ka, pb, kb_, dg=dg, D=D, L=L):
                        r, idx0 = (qb * 128) // L, (qb * 128) % L
                        t0 = idx0 * D + r
                        tsl = slice(t0, t0 + 127 * D + 1, D)
                        for (pp, kk, acc, nm) in ((pa, ka, acc_a, 'acca'), (pb, kb_, acc_b, 'accb')):
                            src = pp[:, 0:256].rearrange("p (a b) -> p a b", a=2)
                            if dg == 0:
                                s.op('dve', lambda e, acc=acc, src=src: e.tensor_copy(out=acc[:, :, tsl], in_=src), reads=[kk], writes=[nm])
                            else:
                                s.op('dve', lambda e, acc=acc, src=src: e.tensor_tensor(out=acc[:, :, tsl], in0=src, in1=acc[:, :, tsl], op=ALU.add),
                                     reads=[kk, nm], writes=[nm])
                    for _ in attn_core(b, L // 128, pv_evac):
                        pass
                    if dg < 2:
                        proj(g, dg + 1)
                    elif g < 3:
                        proj(g + 1, 0)
                for tb8 in range(8):
                    tb = tb8 // 2
                    ts_ = slice(tb8 * 256, (tb8 + 1) * 256)
                    s.op('act', lambda e, ts_=ts_: e.activation(out=acc_a[64:128, :, ts_], in_=acc_a[64:128, :, ts_], func=AF.Ln), reads=['acca'], writes=['acca'])
                    s.op('act', lambda e, ts_=ts_: e.activation(out=reca[0:64, :, :], in_=acc_a[64:128, :, ts_], func=AF.Exp, scale=-1.0), reads=['acca'], writes=['reca'])
                    s.op('pool', lambda e, ts_=ts_: e.tensor_tensor(out=oT[0:64, :, ts_], in0=acc_a[0:64, :, ts_], in1=reca[0:64, :, :], op=ALU.mult),
                         reads=['acca', 'reca'], writes=[('oT', 0, tb), ('oT', 1, tb)])
                    s.op('act', lambda e, ts_=ts_: e.activation(out=acc_b[0:64, :, ts_], in_=acc_b[0:64, :, ts_], func=AF.Ln), reads=['accb'], writes=['accb'])
                    s.op('act', lambda e, ts_=ts_: e.activation(out=reca[64:128, :, :], in_=acc_b[0:64, :, ts_], func=AF.Exp, scale=-1.0), reads=['accb'], writes=['recb'])
                    s.op('pool', lambda e, ts_=ts_: e.tensor_tensor(out=oT[64:128, :, ts_], in0=acc_b[64:128, :, ts_], in1=reca[64:128, :, :], op=ALU.mult),
                         reads=['accb', 'recb'], writes=[('oT', 0, tb), ('oT', 1, tb)])
                if g < 3:
                    for _ in wo_partial(b, cwob[0], ('cwo', 0), g):
                        pass
                else:
                    s.barrier(['ebsem'])
                    nf = Nxt('norm', 4 + i)
                    for _ in wo_partial(b, cwob[0], ('cwo', 0), g, after_tb=lambda tb_: nf.emit(tb_, R, 0), before_tb=lambda tb_: nf.pre(tb_, None, R + 8192)):
                        pass
            s.barrier(['ebsem'])

        def mixer_b(i):
            nf = Nxt('norm', 4 + i)
            bscr = R + 70656
            uT = uv(R, [16, 512], BF16)
            vtok = uv(R + 16384, [4, 2048], F32)
            vn = uv(R + 49152, [4, 2048], BF16)
            stats = uv(R + 65536, [4, 24], F32)
            mv = uv(R + 65536 + 512, [4, 2], F32)
            ttmp = [uv(R + 66560 + j * 2048, [512], F32) for j in range(2)]
            wb2d = bwinb[0]
            tcnt = [0]

            def do_cv(tb):
                def cv(cb, sl, sk):
                    for tc in range(4):
                        p_, k_ = s.ps()
                        t0 = tb * 512 + tc * 128
                        s.mmg(p_[:, :], k_, [(hT[:, kc, t0:t0 + 128], sl[:, kc, :], [sk, ('hT', kc, tb)]) for kc in range(8)])
                        s.op('act', lambda e, tc=tc, cb=cb, p_=p_: e.activation(out=vtok[:, tc, cb * 512:(cb + 1) * 512], in_=p_[:, :], func=AF.Gelu),
                             reads=[k_], writes=[('vtok', tc, cb)])
                        s.op('dve', lambda e, tc=tc, cb=cb: e.bn_stats(out=stats[:, tc, cb * 6:(cb + 1) * 6], in_=vtok[:, tc, cb * 512:(cb + 1) * 512]),
                             reads=[('vtok', tc, cb)], writes=[('stats', tc, cb)])
                stream([(std_tile(wb2d, 0, 2048 + cb * 512), ('bwin', 0)) for cb in range(4)], cv)

            def do_cu(tb):
                tsl = slice(tb * 512, (tb + 1) * 512)

                def cu(cb, sl, sk):
                    for m in range(4):
                        p_, k_ = s.ps()
                        s.mmg(p_[:, :], k_, [(sl[:, kc, m * 128:(m + 1) * 128], hT[:, kc, tsl], [sk, ('hT', kc, tb)]) for kc in range(8)])
                        fc = cb * 4 + m
                        s.op('act', lambda e, fc=fc, p_=p_: e.activation(out=uT[:, fc, :], in_=p_[:, :], func=AF.Gelu), reads=[k_], writes=[('uT', fc)])
                stream([(std_tile(wb2d, 0, cb * 512), ('bwin', 0)) for cb in range(4)], cu)

            def do_ln(tb):
                for tc in range(4):
                    s.op('dve', lambda e, tc=tc: e.bn_aggr(out=mv[:, tc, :], in_=stats[:, tc, :]), reads=[('stats', tc, cb) for cb in range(4)], writes=[('mv', tc)])
                    s.op('act', lambda e, tc=tc: e.activation(out=mv[:, tc, 1:2], in_=mv[:, tc, 1:2], func=AF.Sqrt, bias=LN_EPS, scale=1.0), reads=[('mv', tc)], writes=[('mv', tc)])
                    s.op('dve', lambda e, tc=tc: e.reciprocal(out=mv[:, tc, 1:2], in_=mv[:, tc, 1:2]), reads=[('mv', tc)], writes=[('mv', tc)])
                    s.op('dve', lambda e, tc=tc: e.tensor_scalar(out=vn[:, tc, :], in0=vtok[:, tc, :], scalar1=mv[:, tc, 0:1], scalar2=mv[:, tc, 1:2],
                                                                   op0=ALU.subtract, op1=ALU.mult),
                         reads=[('mv', tc)] + [('vtok', tc, cb) for cb in range(4)], writes=[('vn', tc)])

            def do_spatial(tb):
                for fc in range(16):
                    g = fc // 2
                    p_, k_ = s.ps()
                    for tc in range(4):
                        s.mmg(p_[:, tc * 128:(tc + 1) * 128], k_, [(vn[:, tc, fc * 128:(fc + 1) * 128], WsT[:, g, :], [('vn', tc), ('WsT', g // 4)])])
                    j = tcnt[0] % 2
                    tcnt[0] += 1
                    tt_ = ttmp[j]
                    for tc in range(4):
                        s.op('dve', lambda e, tc=tc, fc=fc, tt_=tt_, p_=p_: e.scalar_tensor_tensor(
                            out=tt_[:, tc * 128:(tc + 1) * 128], in0=p_[:, tc * 128:(tc + 1) * 128], scalar=colprm[:, fc % 8, 9 + fc // 8:10 + fc // 8],
                            in1=Cterm[:, fc, :], op0=ALU.mult, op1=ALU.add), reads=[k_, ('Cterm', fc), 'colprm'], writes=[('tt', j)])
                    s.op('pool', lambda e, fc=fc, tt_=tt_: e.tensor_tensor(out=uT[:, fc, :], in0=tt_[:, :], in1=uT[:, fc, :], op=ALU.mult),
                         reads=[('tt', j), ('uT', fc)], writes=[('uT', fc)])

            def do_out(tb):
                for cb in range(2):
                    pss = [s.ps() for _ in range(4)]

                    def co(kg, sl, sk, cb=cb, pss=pss):
                        for m in range(4):
                            p_, k_ = pss[m]
                            s.mmg(p_[:, :], k_, [(sl[:, kc, m * 128:(m + 1) * 128], uT[:, kg * 8 + kc, :], [sk, ('uT', kg * 8 + kc)]) for kc in range(8)],
                                  first=(kg == 0), last=(kg == 1))
                            if kg == 1:
                                resid_add(cb * 4 + m, tb, p_, k_)
                    stream([(std_tile(bwob[0], kg * 1024, cb * 512), ('bwo', 0)) for kg in range(2)], co)

            do_cv(0)
            do_ln(0)
            do_cu(0)
            do_spatial(0)
            for tb in range(4):
                if tb + 1 < 4:
                    do_cv(tb + 1)
                    do_ln(tb + 1)
                do_out(tb)
                if tb + 1 < 4:
                    do_cu(tb + 1)
                    do_spatial(tb + 1)
                nf.emit(tb, bscr, 0)
            s.barrier()

        xin = [uv(R + 16384 + j * 4096, [1024], F32) for j in range(2)]
        yout = [uv(R + 24576 + j * 4096, [1024], F32) for j in range(2)]

        def load_x(sq_, nxt):
            for tt in range(16):
                j = tt % 2
                s.dma('sp', ('xin', j), [lambda e, tt=tt, j=j: e.dma_start(out=xin[j][:, :], in_=x[sq_, tt * 128:(tt + 1) * 128, :])], reads=[], writes=[('xin', j)], union=True)
                for half in range(2):
                    p_, k_ = s.ps()
                    for q in range(4):
                        c = half * 4 + q
                        s.op('pe', lambda e, j=j, c=c, q=q, p_=p_: e.transpose(out=p_[:, q * 128:(q + 1) * 128], in_=xin[j][:, c * 128:(c + 1) * 128], identity=ident[:]),
                             reads=[('xin', j)], writes=[k_], signal=(q == 3))
                    eng = 'act' if half == 0 else 'dve'
                    if eng == 'act':
                        s.op('act', lambda e, half=half, tt=tt, p_=p_: e.activation(out=xT[:, half * 4:half * 4 + 4, tt * 128:(tt + 1) * 128],
                                                                              in_=p_[:, :].rearrange("p (a b) -> p a b", a=4), func=AF.Copy),
                             reads=[k_], writes=[('xT', half * 4 + q, tt // 4) for q in range(4)])
                    else:
                        s.op('dve', lambda e, half=half, tt=tt, p_=p_: e.tensor_copy(out=xT[:, half * 4:half * 4 + 4, tt * 128:(tt + 1) * 128],
                                                                               in_=p_[:, :].rearrange("p (a b) -> p a b", a=4)),
                             reads=[k_], writes=[('xT', half * 4 + q, tt // 4) for q in range(4)])
                if tt % 4 == 3 and tt // 4 >= 1:
                    nxt.emit(tt // 4 - 1, R, R + 4096)
            nxt.emit(3, R, R + 4096)
            s.barrier([('xin', 0), ('xin', 1), ('yst', 0), ('yst', 1)])

        def store_y(sq_):
            rstd = uv(R, [T], F32)
            rmsnorm_stats(rstd)
            for c in range(8 if DBG >= 8 else 0):
                eng = 'dve'
                s.op(eng, lambda e, c=c: e.scalar_tensor_tensor(out=xT[:, c, :], in0=xT[:, c, :], scalar=colprm[:, c, 8:9], in1=rstd[:, :], op0=ALU.mult, op1=ALU.mult),
                     reads=[('rstd', tb) for tb in range(4)] + ['colprm'] + [('xT', c, tb) for tb in range(4)], writes=[('xT', c, tb) for tb in range(4)])
            for tt in range(16 if DBG >= 9 else 0):
                j = tt % 2
                for half in range(2):
                    p_, k_ = s.ps()
                    for q in range(4):
                        c = half * 4 + q
                        s.op('pe', lambda e, c=c, q=q, tt=tt, p_=p_: e.transpose(out=p_[:, q * 128:(q + 1) * 128], in_=xT[:, c, tt * 128:(tt + 1) * 128], identity=ident[:]),
                             reads=[('xT', c, tt // 4)], writes=[k_], signal=(q == 3))
                    if half == 0:
                        s.op('act', lambda e, j=j, p_=p_: e.activation(out=yout[j][:, 0:512], in_=p_[:, :], func=AF.Copy), reads=[k_], writes=[('yout', j, 0)])
                    else:
                        s.op('dve', lambda e, j=j, p_=p_: e.tensor_copy(out=yout[j][:, 512:1024], in_=p_[:, :]), reads=[k_], writes=[('yout', j, 1)])
                if DBG >= 10:
                  s.dma('sp', ('yst', j), [lambda e, tt=tt, j=j: e.dma_start(out=y[sq_, tt * 128:(tt + 1) * 128, :], in_=yout[j][:, :])],
                      reads=[('yout', j, 0), ('yout', j, 1)], writes=[], union=True)
            s.barrier([('yst', 0), ('yst', 1)])

        for sq_ in range(nseq if DBG >= 6 else 0):
            if sq_ == 0 or not layers:
                load_x(sq_, Nxt('norm', layers[0]) if layers else Nxt('final', sq_))
            for li, i in enumerate(layers):
                kind, j = i % 3, i // 3
                if SKIPMIX:
                    for tb in range(4):
                        norm_tb(4 + i, tb, R)
                    s.barrier()
                elif kind == 0:
                    mixer_a(i, j)
                elif kind == 1:
                    mixer_b(i)
                else:
                    mixer_c(i)
                if li + 1 < len(layers):
                    ffn(i, Nxt('norm', layers[li + 1]))
                else:
                    ffn(i, Nxt('final', sq_, nload=((sq_ + 1, layers[0]) if sq_ + 1 < nseq else None)))
        s.final_wait('pool', [('yst', 0), ('yst', 1)])
        s.final_wait('sp', [('yst', 0), ('yst', 1)])
        s.emit(block)
    return nc


_OH = None


def _consts():
    global _OH
    if _OH is None:
        _OH = (np.eye(128, dtype=np.float32), np.ascontiguousarray(_onehot_tables().reshape(33, 4 * 512)))
    return _OH


def make_in_maps(inputs, seq_lists):
    ident, oh = _consts()
    f = lambda a: np.ascontiguousarray(np.asarray(a, dtype=np.float32))
    prm = np.concatenate([f(inputs["norm_mix_g"]), f(inputs["norm_ffn_g"]), f(inputs["final_g"]).reshape(1, 1024),
                          f(inputs["b_ln_g"]).reshape(2, 1024), f(inputs["b_ln_b"]).reshape(2, 1024)], axis=0)
    shared = {
        "rel_bias": f(inputs["rel_bias"]), "prm": np.ascontiguousarray(prm),
        "ffn_w1": f(inputs["ffn_w1"]), "ffn_w2": f(inputs["ffn_w2"]),
        "a_wqkv": f(inputs["a_wqkv"]), "a_sink": f(inputs["a_sink"]).reshape(1, 32), "a_wo": f(inputs["a_wo"]),
        "b_win": f(inputs["b_win"]), "b_ws": f(inputs["b_ws"]).reshape(8, 128, 128), "b_bs": f(inputs["b_bs"]).reshape(1, 1024),
        "b_wo": f(inputs["b_wo"]), "c_wqkv": f(inputs["c_wqkv"]), "c_wo": f(inputs["c_wo"]),
        "ident": ident, "oh": oh,
    }
    xp, xs = inputs["x_prompt"], inputs["x_sample"]
    maps = []
    for sl in seq_lists:
        xs_ = np.stack([np.asarray(xp[b] if grp == 0 else xs[b], dtype=np.float32) for grp, b in sl], axis=0)
        m = dict(shared)
        m["x"] = np.ascontiguousarray(xs_)
        maps.append(m)
    return maps


def kernel(**inputs):
    seq_lists = [[(0, 2 * c), (0, 2 * c + 1), (1, c)] for c in range(NCORES)]
    nc = build(3)
    maps = make_in_maps(inputs, seq_lists)
    res = run_bass_kernel_spmd(nc, maps, core_ids=list(range(NCORES)))
    yp = np.empty((16, T, 1024), np.float32)
    ys = np.empty((8, T, 1024), np.float32)
    for c in range(NCORES):
        yc = np.asarray(res.results[c]["y"])
        yp[2 * c] = yc[0]
        yp[2 * c + 1] = yc[1]
        ys[c] = yc[2]
    return (yp, ys)
```
